# Optimizing a Trainium2 kernel written in Bass

```python
import jax, jax.numpy as jnp
from jax import lax
import numpy as np

D_MODEL = 1024
BATCH = 32
SEQ = 2048
DEPTH = 4

BRANCH_WIDTH = D_MODEL // 2
N_BRANCH = 3
GM_CHUNK = 128
GM_GROUPS = 4
GM_GROUP_WIDTH = BRANCH_WIDTH // GM_GROUPS
SB_HEAD_DIM = 64
SB_HEADS = BRANCH_WIDTH // SB_HEAD_DIM
SB_QBLOCK = 128
POOL_WINDOWS = (2, 4, 8, 16)
POOL_GROUPS = len(POOL_WINDOWS)
POOL_GROUP_WIDTH = BRANCH_WIDTH // POOL_GROUPS
D_FF = -(-8 * D_MODEL // (3 * 256)) * 256
N_MOD = 6
EPS = 1e-6
IN_SIZES = (BRANCH_WIDTH,) * 6 + (N_BRANCH * D_MODEL,)
IN_SPLITS = tuple(int(s) for s in np.cumsum(IN_SIZES)[:-1])
IN_COLS = int(sum(IN_SIZES))

kernel_name = "hybrid_gmlp_stickbreak_pool_adaln"


def rmsnorm(x, g):
    xf = x.astype(jnp.float32)
    xf = xf * lax.rsqrt(jnp.mean(xf * xf, axis=-1, keepdims=True) + EPS)
    return xf.astype(x.dtype) * g


def layernorm(x, g, b):
    xf = x.astype(jnp.float32)
    mu = jnp.mean(xf, axis=-1, keepdims=True)
    xc = xf - mu
    xf = xc * lax.rsqrt(jnp.mean(xc * xc, axis=-1, keepdims=True) + EPS)
    return xf.astype(x.dtype) * g + b


def gmlp_mixer(u, v, ln_g, ln_b, w_s, b_s):
    B, S, _ = v.shape
    v = layernorm(v, ln_g, ln_b)
    vc = v.reshape(B, S // GM_CHUNK, GM_CHUNK, GM_GROUPS, GM_GROUP_WIDTH)
    causal = jnp.tril(jnp.ones((GM_CHUNK, GM_CHUNK), dtype=bool))
    w = jnp.where(causal[None], w_s, 0.0)
    s = jnp.einsum('gts,bnsgc->bntgc', w, vc) + b_s.T[:, :, None]
    return u * s.reshape(B, S, BRANCH_WIDTH)


def stick_breaking_attention(q, k, v):
    B, S, _ = q.shape
    to_heads = lambda a: a.reshape(B, S, SB_HEADS, SB_HEAD_DIM).transpose(0, 2, 1, 3)
    q, k, v = to_heads(q), to_heads(k), to_heads(v)
    scale = SB_HEAD_DIM ** -0.5
    outs = []
    for i in range(S // SB_QBLOCK):
        start, end = i * SB_QBLOCK, (i + 1) * SB_QBLOCK
        qb = q[:, :, start:end].astype(jnp.float32)
        kb = k[:, :, :end].astype(jnp.float32)
        z = jnp.einsum('bhqd,bhkd->bhqk', qb, kb) * scale
        t_pos = start + jnp.arange(SB_QBLOCK)[:, None]
        s_pos = jnp.arange(end)[None, :]
        mask = s_pos < t_pos
        log_beta = jax.nn.log_sigmoid(z)
        log_one_minus = jnp.where(mask, log_beta - z, 0.0)
        suffix = lax.cumsum(log_one_minus, axis=3, reverse=True) - log_one_minus
        a = jnp.where(mask, jnp.exp(log_beta + suffix), 0.0)
        outs.append(jnp.einsum('bhqk,bhkd->bhqd', a.astype(v.dtype), v[:, :, :end]))
    o = jnp.concatenate(outs, axis=2)
    return o.transpose(0, 2, 1, 3).reshape(B, S, BRANCH_WIDTH)


def pool_mixer(xp, w_pool, pool_scale):
    B, S, _ = xp.shape
    xg = xp.astype(jnp.float32).reshape(B, S, POOL_GROUPS, POOL_GROUP_WIDTH)
    prefix = jnp.cumsum(xg, axis=1)
    pos = jnp.arange(S)
    diffs = []
    for g, w in enumerate(POOL_WINDOWS):
        pg = prefix[:, :, g]
        lagged = jnp.pad(pg[:, :S - w], ((0, 0), (w, 0), (0, 0)))
        count = jnp.minimum(pos + 1, w).astype(jnp.float32)[None, :, None]
        diffs.append((pg - lagged) / count - xg[:, :, g])
    d = jnp.stack(diffs, axis=2).astype(xp.dtype)
    y = jnp.einsum('bsgc,gcd->bsgd', d, w_pool)
    return y.reshape(B, S, BRANCH_WIDTH) * pool_scale


def setup_inputs(seed: int = 0) -> dict:
    key = jax.random.key(seed)
    ks = jax.random.split(key, 20)
    f32 = jnp.float32
    nrm = lambda k, shape, s: jax.random.normal(k, shape, f32) * s
    L, D, W = DEPTH, D_MODEL, BRANCH_WIDTH
    return {
        "x": nrm(ks[0], (BATCH, SEQ, D), 1.0),
        "c": nrm(ks[1], (BATCH, D), 1.0),
        "rms_g1": 1.0 + nrm(ks[2], (L, D), 0.02),
        "rms_g2": 1.0 + nrm(ks[3], (L, D), 0.02),
        "w_ada": nrm(ks[4], (L, D, N_MOD * D), 0.5 * D ** -0.5),
        "b_ada": nrm(ks[5], (L, N_MOD * D), 0.02),
        "w_in": nrm(ks[6], (L, D, IN_COLS), D ** -0.5),
        "gm_ln_g": 1.0 + nrm(ks[7], (L, W), 0.02),
        "gm_ln_b": nrm(ks[8], (L, W), 0.02),
        "gm_w_spatial": nrm(ks[9], (L, GM_GROUPS, GM_CHUNK, GM_CHUNK), GM_CHUNK ** -0.5),
        "gm_b_spatial": 1.0 + nrm(ks[10], (L, GM_GROUPS, GM_CHUNK), 0.02),
        "pool_w": nrm(ks[11], (L, POOL_GROUPS, POOL_GROUP_WIDTH, POOL_GROUP_WIDTH), POOL_GROUP_WIDTH ** -0.5),
        "pool_scale": 1.0 + nrm(ks[12], (L, W), 0.02),
        "w_branch": nrm(ks[13], (L, N_BRANCH, W, D), W ** -0.5),
        "w_out": nrm(ks[14], (L, D, D), D ** -0.5),
        "w_ffn_in": nrm(ks[15], (L, D, 2 * D_FF), D ** -0.5),
        "w_ffn_out": nrm(ks[16], (L, D_FF, D), D_FF ** -0.5),
        "final_g": 1.0 + nrm(ks[17], (D,), 0.02),
    }


def reference(x, c, rms_g1, rms_g2, w_ada, b_ada, w_in, gm_ln_g, gm_ln_b, gm_w_spatial, gm_b_spatial,
              pool_w, pool_scale, w_branch, w_out, w_ffn_in, w_ffn_out, final_g):
    B, S, D = x.shape
    c_act = jax.nn.silu(c)
    for l in range(DEPTH):
        mod = c_act @ w_ada[l] + b_ada[l]
        sh1, sc1, gt1, sh2, sc2, gt2 = [m[:, None, :] for m in jnp.split(mod, N_MOD, axis=-1)]

        h = rmsnorm(x, rms_g1[l]) * (1.0 + sc1) + sh1
        proj = h @ w_in[l]
        gm_u, gm_v, sb_q, sb_k, sb_v, pool_in, gate_logits = jnp.split(proj, IN_SPLITS, axis=-1)
        branches = (
            gmlp_mixer(jax.nn.gelu(gm_u), jax.nn.gelu(gm_v), gm_ln_g[l], gm_ln_b[l],
                       gm_w_spatial[l], gm_b_spatial[l]),
            stick_breaking_attention(sb_q, sb_k, sb_v),
            pool_mixer(pool_in, pool_w[l], pool_scale[l]),
        )
        gates = jax.nn.sigmoid(gate_logits.reshape(B, S, N_BRANCH, D))
        merged = sum(gates[:, :, n] * (branches[n] @ w_branch[l, n]) for n in range(N_BRANCH))
        x = x + gt1 * (merged @ w_out[l])

        h2 = rmsnorm(x, rms_g2[l]) * (1.0 + sc2) + sh2
        f_gate, f_up = jnp.split(h2 @ w_ffn_in[l], 2, axis=-1)
        x = x + gt2 * ((jax.nn.silu(f_gate) * f_up) @ w_ffn_out[l])
    return rmsnorm(x, final_g)
```

```python
import numpy as np
from contextlib import ExitStack
import concourse.bass as bass
import concourse.mybir as mybir
from concourse.bass_utils import run_bass_kernel_spmd

F32 = mybir.dt.float32
BF16 = mybir.dt.bfloat16
AF = mybir.ActivationFunctionType
ALU = mybir.AluOpType

D = 1024
SEQ = 2048
DEPTH = 4
BW = 512
DFF = 2816
NMOD = 6
T = 512
NG = SEQ // T
EPS = 1e-6
NCORES = 8
SLOT_EL = 4608
NSLOT = 3
NEG = -30000.0

SLABS = [4096] * 6 + [4608] * 8 + [4096] * 2 + [4096] * 11 + [2816] * 8
SLAB_OFF = np.concatenate([[0], np.cumsum(SLABS)]).astype(np.int64)
WEL = int(SLAB_OFF[-1])
NSLAB = len(SLABS)

P_G1 = 0
P_G2 = 32
P_FG = 64
P_BADA = 72
P_PSC = 264
P_CT = 280
P_INVC = 312
NPRM = 376


class Buf:
    __slots__ = ("name", "w", "r", "alias")

    def __init__(self, name):
        self.name = name
        self.w = None
        self.r = {}
        self.alias = []


def alias(la, lb):
    for a in la:
        for b in lb:
            a.alias.append(b)
            b.alias.append(a)


class Rec:
    def __init__(self):
        self.call = None

    def __getattr__(self, name):
        def f(*a, **k):
            self.call = (name, a, k)
            return self
        return f


def _rec(fn):
    r = Rec()
    fn(r)
    assert r.call is not None
    return r.call


class Eng:
    def __init__(self, name, sem):
        self.name = name
        self.sem = sem
        self.cnt = 0
        self.seen = {}
        self.prog = []

    def _need(self, waits, ev, same_ok):
        if ev is None:
            return
        key, val, semh = ev
        if same_ok and key == self.name:
            return
        if self.seen.get(key, 0) >= val:
            return
        self.seen[key] = val
        waits.append((semh, val))

    def _deps(self, r, w):
        waits = []
        for b in r:
            self._need(waits, b.w, False)
        same_ok = (self.name == "pe")
        for b in w:
            for bb in [b] + b.alias:
                self._need(waits, bb.w, same_ok)
                for key, (val, semh) in bb.r.items():
                    self._need(waits, (key, val, semh), same_ok)
        return waits

    def _commit(self, ev, r, w):
        for b in r:
            cur = b.r.get(ev[0])
            if cur is None or cur[0] < ev[1]:
                b.r[ev[0]] = (ev[1], ev[2])
        for b in w:
            b.w = ev
            b.r = {}
            for bb in b.alias:
                bb.r = {}

    def op(self, fn, r=(), w=(), sig=True):
        waits = self._deps(r, w)
        if sig:
            self.cnt += 1
            ev = (self.name, self.cnt, self.sem)
            self.prog.append((waits, _rec(fn), self.sem, 1))
        else:
            ev = (self.name, self.cnt + 1, self.sem)
            self.prog.append((waits, _rec(fn), None, 0))
        self._commit(ev, r, w)

    def dma(self, fn, dsem, r=(), w=()):
        waits = self._deps(r, w)
        dsem[2] += 16
        ev = (dsem[0], dsem[2], dsem[1])
        self.prog.append((waits, _rec(fn), dsem[1], 16))
        self._commit(ev, r, w)

    def wait_for(self, bufs):
        waits = []
        for b in bufs:
            self._need(waits, b.w, False)
            for key, (val, semh) in b.r.items():
                self._need(waits, (key, val, semh), False)
        self.prog.append((waits, None, None, 0))

    def replay(self, e):
        for waits, fn, sem, inc in self.prog:
            for semh, val in waits:
                e.wait_ge(semh, val)
            if fn is None:
                continue
            inst = getattr(e, fn[0])(*fn[1], **fn[2])
            if sem is not None:
                inst.then_inc(sem, inc)


def build_program(nseq, layers, do_final=True, dbg=False):
    nc = bass.Bass("TRN2", target_bir_lowering=False)
    x_d = nc.dram_tensor("x", [nseq, SEQ, D], F32, kind="ExternalInput").ap()
    y_d = nc.dram_tensor("y", [nseq, SEQ, D], F32, kind="ExternalOutput").ap()
    wst_d = nc.dram_tensor("wst", [DEPTH, 128, WEL], F32, kind="ExternalInput").ap()
    wada_d = nc.dram_tensor("wada", [DEPTH, 128, 8 * NMOD * D], F32, kind="ExternalInput").ap()
    prm_d = nc.dram_tensor("prm", [128, NPRM], F32, kind="ExternalInput").ap()
    cm_d = nc.dram_tensor("cm", [128, 5 * 128], F32, kind="ExternalInput").ap()
    wsT_d = nc.dram_tensor("wsT", [128, DEPTH * 4 * 128], F32, kind="ExternalInput").ap()
    plw_d = nc.dram_tensor("plw", [128, DEPTH * 4 * 128], F32, kind="ExternalInput").ap()
    bsr_d = nc.dram_tensor("bsr", [1, DEPTH * 4 * 128], F32, kind="ExternalInput").ap()
    lnb_d = nc.dram_tensor("lnb", [DEPTH, 128, 2 * 512], F32, kind="ExternalInput").ap()
    if dbg:
        dbg_u = nc.dram_tensor("dbg_u", [128, 4 * T], BF16, kind="ExternalOutput").ap()
        dbg_q = nc.dram_tensor("dbg_q", [128, 4 * T], BF16, kind="ExternalOutput").ap()
        dbg_m = nc.dram_tensor("dbg_m", [128, 8 * T], BF16, kind="ExternalOutput").ap()

    es = ExitStack()
    with es:
        def sb(name, shape, dt):
            return es.enter_context(nc.sbuf_tensor(name, shape, dt))

        def mksem(name):
            return es.enter_context(nc.semaphore(name))

        xT = sb("xT", [128, 8, SEQ], F32)
        kT = sb("kT", [128, 4, SEQ], BF16)
        vC = sb("vC", [128, 16, BW], BF16)
        slots = [sb(f"slot{i}", [128, SLOT_EL], BF16) for i in range(NSLOT)]
        identf = sb("identf", [128, 128], F32)
        identb = sb("identb", [128, 128], BF16)
        onesb = sb("onesb", [128, 128], BF16)
        negones = sb("negones", [128, 128], BF16)
        NTb = sb("NTb", [128, 128], BF16)
        negmask = sb("negmask", [128, 128], BF16)
        onesrow = sb("onesrow", [1, 128], BF16)
        bsrow = sb("bsrow", [1, DEPTH * 4 * 128], BF16)
        WT = sb("WT", [128, DEPTH * 4, 128], BF16)
        poolw = sb("poolw", [128, DEPTH * 4, 128], BF16)
        prm = sb("prm_sb", [128, NPRM], F32)
        mod = sb("mod", [128, DEPTH, 48, 4], F32)
        mh = sb("mh", [128, T], F32)
        lnb = sb("lnb_sb", [128, 2, 512], F32)
        cact = sb("cact", [128, 8, 4], BF16)
        gs = sb("gs", [128, 2, 8], F32)
        hT = sb("hT", [128, 8, T], BF16)
        dT = sb("dT", [128, 4, T], BF16)
        stats = sb("stats", [128, 4, 8], F32)
        carry = sb("carry", [128, 4, 16], F32)
        arX = sb("arX", [128, 26624], mybir.dt.uint8)
        arY = sb("arY", [128, 16896], mybir.dt.uint8)
        ps = es.enter_context(nc.psum_tensor("ps", [128, 8, 512], F32))

        def view(ar, off, n, dt):
            esz = 4 if dt == F32 else 2
            return ar[:, off:off + n * esz].bitcast(dt)

        uT = view(arX, 0, 4 * T, BF16)
        gv = view(arX, 4096, 2 * T, F32)
        vln = view(arX, 8192, 4 * T, BF16)
        qT = view(arX, 12288, 4 * T, BF16)
        att0 = 16384
        Ebuf = view(arX, att0, 2 * T, F32)
        Sbuf = view(arX, att0 + 4096, 2 * T, BF16)
        Abuf = view(arX, att0 + 6144, 2 * T, BF16)
        Lbuf = view(arX, att0 + 8192, 2 * T, BF16)
        sqb = view(arX, att0, 2 * T, BF16)
        rstd = view(arX, att0 + 2048, T, F32)
        tmpn = view(arX, att0 + 4096, 2 * T, F32)
        ffT = view(arX, 0, 22 * T, BF16)
        silb = view(arX, 22528, 2 * T, F32)
        yT = view(arX, 0, 8 * T, F32)
        cmf = view(arX, 0, 5 * 128, F32)
        wsTf = view(arX, 4096, DEPTH * 4 * 128, F32)
        plwf = view(arX, 4096 + 8192, DEPTH * 4 * 128, F32)
        Pp = view(arY, 0, 4 * 528, F32)
        ptmp = view(arY, 8448, 2 * 528, F32)
        gsg = view(arY, 0, 2 * T, F32)
        mrg_m = view(arY, 4096, T, F32)
        mrg_t = view(arY, 6144, T, F32)
        mrgT = view(arY, 8192, 8 * T, BF16)
        xin = view(arY, 0, 2 * 1024, F32)
        bsf = view(arY, 8192, DEPTH * 4 * 128, F32)

        B = {}

        def nb(name, n=None):
            if n is None:
                B[name] = Buf(name)
            else:
                B[name] = [Buf(f"{name}{i}") for i in range(n)]
            return B[name]

        b_xT = nb("xT", 8 * NG)
        b_kT = nb("kT", 4 * NG)
        b_vC = nb("vC", 16)
        b_slot = nb("slot", NSLOT)
        b_const = nb("const")
        b_mod = nb("mod")
        b_lnb = nb("lnb")
        b_gs = nb("gs")
        b_h = nb("h", 8)
        b_d = nb("d", 4)
        b_stats = nb("stats", 4)
        b_carry = nb("carry")
        b_ps = nb("ps", 8)
        b_u = nb("u", 4)
        b_gv = nb("gv", 2)
        b_vln = nb("vln", 4)
        b_q = nb("q", 4)
        b_E = nb("E", 2)
        b_S = nb("S", 2)
        b_A = nb("A", 2)
        b_L = nb("L", 2)
        b_sq = nb("sq", 2)
        b_rstd = nb("rstd")
        b_tmpn = nb("tmpn", 2)
        b_ff = nb("ff", 22)
        b_sil = nb("sil", 2)
        b_yT = nb("yT", 8)
        b_P = nb("P", 4)
        b_ptmp = nb("ptmp", 2)
        b_gsg = nb("gsg", 2)
        b_mm = nb("mm")
        b_mt = nb("mt")
        b_mrg = nb("mrg", 8)
        b_xin = nb("xin", 2)
        b_proX = nb("proX")
        b_proY = nb("proY")
        att_b = b_E + b_S + b_A + b_L
        nrm_b = b_sq + [b_rstd] + b_tmpn
        st1X = b_u + b_gv + b_vln + b_q + att_b + nrm_b
        st3X = b_ff + b_sil
        alias(att_b, nrm_b)
        alias(st1X, st3X)
        alias(st1X + st3X, b_yT + [b_proX])
        alias(b_yT, [b_proX])
        st1Y = b_P + b_ptmp
        st2Y = b_gsg + [b_mm, b_mt] + b_mrg
        alias(st1Y, st2Y)
        alias(st1Y + st2Y, b_xin + [b_proY])
        alias(b_xin, [b_proY])

        pe = Eng("pe", mksem("s_pe"))
        act = Eng("act", mksem("s_act"))
        dve = Eng("dve", mksem("s_dve"))
        pool = Eng("pool", mksem("s_pool"))
        sp = Eng("sp", mksem("s_sp"))
        ds_slot = [[f"dslot{i}", mksem(f"d_slot{i}"), 0] for i in range(NSLOT)]
        ds_c = ["dc", mksem("d_c"), 0]
        ds_xin = [[f"dxin{i}", mksem(f"d_xin{i}"), 0] for i in range(2)]
        ds_out = [[f"dout{i}", mksem(f"d_out{i}"), 0] for i in range(2)]
        ds_lnb = ["dlnb", mksem("d_lnb"), 0]
        ds_dbg = [[f"ddbg{i}", mksem(f"d_dbg{i}"), 0] for i in range(3)]

        wplan = []
        for l in layers:
            for s_ in range(12):
                wplan.append((lambda l=l, s_=s_: wada_d[l, :, s_ * 4096:(s_ + 1) * 4096], 4096))
        for b in range(nseq):
            for l in layers:
                for Q in range(NG):
                    for s_ in range(NSLAB):
                        o0, o1 = int(SLAB_OFF[s_]), int(SLAB_OFF[s_ + 1])
                        wplan.append((lambda l=l, o0=o0, o1=o1: wst_d[l, :, o0:o1], o1 - o0))
        wstate = {"issued": 0, "next": 0}

        def w_issue():
            i = wstate["issued"]
            apf, nel = wplan[i]
            si = i % NSLOT
            src = apf()
            dst = slots[si][:, 0:nel]
            pool.dma(lambda e, dst=dst, src=src: e.dma_start(out=dst, in_=src), ds_slot[si], r=(), w=[b_slot[si]])
            wstate["issued"] = i + 1

        def w_get():
            i = wstate["next"]
            while wstate["issued"] < min(i + NSLOT, len(wplan)):
                w_issue()
            wstate["next"] = i + 1
            si = i % NSLOT
            return slots[si], b_slot[si]

        psrot = {"i": 0}

        def ps_next(lo=0, hi=8):
            i = psrot["i"]
            psrot["i"] = i + 1
            k = lo + i % (hi - lo)
            return ps[:, k, :], b_ps[k]

        sp.dma(lambda e: e.dma_start(out=prm[:], in_=prm_d[:, :]), ds_c, w=[b_const])
        sp.dma(lambda e: e.dma_start(out=cmf, in_=cm_d[:, :]), ds_c, w=[b_proX])
        sp.dma(lambda e: e.dma_start(out=wsTf, in_=wsT_d[:, :]), ds_c, w=[b_proX])
        sp.dma(lambda e: e.dma_start(out=plwf, in_=plw_d[:, :]), ds_c, w=[b_proX])
        sp.dma(lambda e: e.dma_start(out=bsf[0:1, :], in_=bsr_d[:, :]), ds_c, w=[b_proY])
        b_const.w = b_proX.w = b_proY.w
        cw = [b_const]
        dve.op(lambda e: e.tensor_copy(out=identf[:], in_=cmf[:, 0:128]), r=[b_proX], w=cw)
        dve.op(lambda e: e.tensor_copy(out=identb[:], in_=cmf[:, 0:128]), r=[b_proX], w=cw)
        dve.op(lambda e: e.tensor_copy(out=onesb[:], in_=cmf[:, 128:256]), r=[b_proX], w=cw)
        dve.op(lambda e: e.tensor_copy(out=NTb[:], in_=cmf[:, 256:384]), r=[b_proX], w=cw)
        dve.op(lambda e: e.tensor_copy(out=negmask[:], in_=cmf[:, 384:512]), r=[b_proX], w=cw)
        dve.op(lambda e: e.tensor_scalar(out=negones[:], in0=cmf[:, 128:256], scalar1=-1.0, scalar2=None,
                                         op0=ALU.mult), r=[b_proX], w=cw)
        dve.op(lambda e: e.tensor_copy(out=onesrow[:], in_=cmf[0:1, 128:256]), r=[b_proX], w=cw)
        for lg in range(DEPTH * 4):
            dve.op(lambda e, lg=lg: e.tensor_tensor(out=WT[:, lg, :], in0=wsTf[:, lg * 128:(lg + 1) * 128],
                                                    in1=cmf[:, 512:640], op=ALU.mult), r=[b_proX], w=cw)
        dve.op(lambda e: e.tensor_copy(out=poolw[:].rearrange("p a b -> p (a b)"), in_=plwf), r=[b_proX], w=cw)
        dve.op(lambda e: e.tensor_copy(out=bsrow[:], in_=bsf[0:1, :]), r=[b_proY], w=cw)
        pool.op(lambda e: e.memset(mh[:], -0.5), w=cw)
        pool.op(lambda e: e.memset(carry[:], 0.0), w=[b_carry])
        act.op(lambda e: e.activation(out=cact[:].rearrange("p a b -> p (a b)"), in_=prm[:, P_CT:P_CT + 32],
                                      func=AF.Silu), r=[b_const], w=cw)
        for li, l in enumerate(layers):
            for s_ in range(12):
                slot, bsl = w_get()
                for cb in range(4):
                    pt, bpt = ps_next()
                    for k in range(8):
                        pe.op(lambda e, pt=pt, slot=slot, k=k, cb=cb: e.matmul(
                            pt[:, 0:4], lhsT=slot[:, k * 512 + cb * 128:k * 512 + (cb + 1) * 128],
                            rhs=cact[:, k, :], start=(k == 0), stop=(k == 7)),
                            r=[bsl, b_const], w=[bpt], sig=(k == 7))
                    col = s_ * 4 + cb
                    dve.op(lambda e, pt=pt, l=l, col=col: e.tensor_scalar(
                        out=mod[:, l, col, :], in0=pt[:, 0:4],
                        scalar1=prm[:, P_BADA + l * 48 + col:P_BADA + l * 48 + col + 1], scalar2=None,
                        op0=ALU.add), r=[bpt, b_const], w=[b_mod])

        def norm(Q, gvec, shvec, out_fn, out_bufs, xsrc=None):
            c0 = Q * T
            pt, bpt = ps_next()
            for k in range(8):
                j = k % 2
                pool.op(lambda e, k=k, j=j: e.tensor_tensor(out=sqb[:, j * T:(j + 1) * T], in0=xT[:, k, c0:c0 + T],
                                                            in1=xT[:, k, c0:c0 + T], op=ALU.mult),
                        r=[b_xT[k * NG + Q]], w=[b_sq[j]])
                pe.op(lambda e, k=k, j=j, pt=pt: e.matmul(pt, lhsT=onesb[:], rhs=sqb[:, j * T:(j + 1) * T],
                                                         start=(k == 0), stop=(k == 7)),
                      r=[b_sq[j], b_const], w=[bpt], sig=True)
            dve.op(lambda e, pt=pt: e.tensor_scalar(out=rstd, in0=pt, scalar1=1.0 / D, scalar2=EPS,
                                                    op0=ALU.mult, op1=ALU.add), r=[bpt], w=[b_rstd])
            pool.op(lambda e: e.tensor_tensor(out=rstd, in0=rstd, in1=mh[:], op=ALU.pow),
                    r=[b_rstd, b_const], w=[b_rstd])
            for k in range(8):
                j = k % 2
                if shvec is None:
                    dve.op(lambda e, k=k: e.scalar_tensor_tensor(
                        out=out_fn(k), in0=xT[:, k, c0:c0 + T], scalar=gvec(k), in1=rstd,
                        op0=ALU.mult, op1=ALU.mult), r=[b_xT[k * NG + Q], b_rstd, b_const], w=[out_bufs[k]])
                else:
                    dve.op(lambda e, k=k, j=j: e.scalar_tensor_tensor(
                        out=tmpn[:, j * T:(j + 1) * T], in0=xT[:, k, c0:c0 + T], scalar=gvec(k), in1=rstd,
                        op0=ALU.mult, op1=ALU.mult), r=[b_xT[k * NG + Q], b_rstd, b_gs], w=[b_tmpn[j]])
                    pool.op(lambda e, k=k, j=j: e.tensor_scalar(
                        out=out_fn(k), in0=tmpn[:, j * T:(j + 1) * T], scalar1=shvec(k), scalar2=None,
                        op0=ALU.add), r=[b_tmpn[j], b_mod], w=[out_bufs[k]])

        def fm_proj(slot, bsl, cbs, rhs_fn, rhs_bufs, nk, wcol_fn):
            outs = []
            for cb in cbs:
                pt, bpt = ps_next()
                for k in range(nk):
                    pe.op(lambda e, pt=pt, k=k, cb=cb: e.matmul(pt, lhsT=wcol_fn(k, cb), rhs=rhs_fn(k),
                                                               start=(k == 0), stop=(k == nk - 1)),
                          r=[bsl, rhs_bufs[k]], w=[bpt], sig=(k == nk - 1))
                outs.append((pt, bpt))
            return outs

        for b in range(nseq):
            for tb in range(16):
                j = tb % 2
                sp.dma(lambda e, tb=tb, j=j: e.dma_start(out=xin[:, j * 1024:(j + 1) * 1024],
                                                         in_=x_d[b, tb * 128:(tb + 1) * 128, :]),
                       ds_xin[j], w=[b_xin[j]])
                for kh in range(2):
                    pt, bpt = ps_next()
                    for kk in range(4):
                        k = kh * 4 + kk
                        pe.op(lambda e, pt=pt, kk=kk, k=k, j=j: e.transpose(
                            pt[:, kk * 128:(kk + 1) * 128], xin[:, j * 1024 + k * 128:j * 1024 + (k + 1) * 128],
                            identf[:]), r=[b_xin[j], b_const], w=[bpt], sig=(kk == 3))
                    Qx = tb // 4
                    wl = [b_xT[(kh * 4 + kk) * NG + Qx] for kk in range(4)]
                    eng = dve if kh == 0 else act
                    if kh == 0:
                        dve.op(lambda e, pt=pt, kh=kh, tb=tb: e.tensor_copy(
                            out=xT[:, kh * 4:(kh + 1) * 4, tb * 128:(tb + 1) * 128],
                            in_=pt.rearrange("p (a b) -> p a b", a=4)), r=[bpt], w=wl)
                    else:
                        act.op(lambda e, pt=pt, kh=kh, tb=tb: e.activation(
                            out=xT[:, kh * 4:(kh + 1) * 4, tb * 128:(tb + 1) * 128],
                            in_=pt.rearrange("p (a b) -> p a b", a=4), func=AF.Copy), r=[bpt], w=wl)

            for l in layers:
                MB = lambda c: mod[:, l, c, b:b + 1]
                for t_, (pg, sc0) in enumerate(((P_G1, 8), (P_G2, 32))):
                    dve.op(lambda e, t_=t_, pg=pg, sc0=sc0: e.scalar_tensor_tensor(
                        out=gs[:, t_, :], in0=mod[:, l, sc0:sc0 + 8, b], scalar=1.0,
                        in1=prm[:, pg + l * 8:pg + l * 8 + 8], op0=ALU.add, op1=ALU.mult),
                        r=[b_mod, b_const], w=[b_gs])
                sp.dma(lambda e: e.dma_start(out=lnb[:].rearrange("p a b -> p (a b)"), in_=lnb_d[l, :, :]),
                       ds_lnb, w=[b_lnb])
                pool.op(lambda e: e.memset(carry[:], 0.0), w=[b_carry])

                for Q in range(NG):
                    c0 = Q * T
                    norm(Q, lambda k: gs[:, 0, k:k + 1], lambda k: MB(0 + k),
                         lambda k: hT[:, k, :], b_h)
                    hr = lambda k: hT[:, k, :]

                    slot, bsl = w_get()
                    outs = fm_proj(slot, bsl, range(4), hr, b_h, 8,
                                   lambda k, cb, slot=slot: slot[:, k * 512 + cb * 128:k * 512 + (cb + 1) * 128])
                    for cb, (pt, bpt) in enumerate(outs):
                        act.op(lambda e, pt=pt, cb=cb: e.activation(out=uT[:, cb * T:(cb + 1) * T], in_=pt,
                                                                    func=AF.Gelu_apprx_tanh),
                               r=[bpt], w=[b_u[cb]])
                    slot, bsl = w_get()
                    for tb in range(4):
                        pt, bpt = ps_next()
                        j = tb % 2
                        for k in range(8):
                            pe.op(lambda e, pt=pt, k=k, tb=tb, slot=slot: e.matmul(
                                pt, lhsT=hT[:, k, tb * 128:(tb + 1) * 128], rhs=slot[:, k * 512:(k + 1) * 512],
                                start=(k == 0), stop=(k == 7)), r=[bsl, b_h[k]], w=[bpt], sig=(k == 7))
                        gvj = gv[:, j * T:(j + 1) * T]
                        act.op(lambda e, pt=pt, gvj=gvj: e.activation(out=gvj, in_=pt, func=AF.Gelu_apprx_tanh),
                               r=[bpt], w=[b_gv[j]])
                        dve.op(lambda e, gvj=gvj, tb=tb: e.bn_stats(out=stats[:, tb, 0:6], in_=gvj),
                               r=[b_gv[j]], w=[b_stats[tb]])
                        dve.op(lambda e, tb=tb: e.bn_aggr(out=stats[:, tb, 6:8], in_=stats[:, tb, 0:6]),
                               r=[b_stats[tb]], w=[b_stats[tb]])
                        dve.op(lambda e, tb=tb: e.tensor_scalar(out=stats[:, tb, 7:8], in0=stats[:, tb, 7:8],
                                                                scalar1=EPS, scalar2=None, op0=ALU.add),
                               r=[b_stats[tb]], w=[b_stats[tb]])
                        pool.op(lambda e, tb=tb: e.tensor_tensor(out=stats[:, tb, 7:8], in0=stats[:, tb, 7:8],
                                                                 in1=mh[:, 0:1], op=ALU.pow),
                                r=[b_stats[tb], b_const], w=[b_stats[tb]])
                        dve.op(lambda e, gvj=gvj, tb=tb: e.tensor_scalar(
                            out=gvj, in0=gvj, scalar1=stats[:, tb, 6:7], scalar2=stats[:, tb, 7:8],
                            op0=ALU.subtract, op1=ALU.mult), r=[b_gv[j], b_stats[tb]], w=[b_gv[j]])
                        dve.op(lambda e, gvj=gvj: e.tensor_tensor(out=gvj, in0=gvj, in1=lnb[:, 0, :], op=ALU.mult),
                               r=[b_gv[j], b_lnb], w=[b_gv[j]])
                        pool.op(lambda e, gvj=gvj, tb=tb: e.tensor_tensor(
                            out=vln[:, tb * T:(tb + 1) * T], in0=gvj, in1=lnb[:, 1, :], op=ALU.add),
                            r=[b_gv[j], b_lnb], w=[b_vln[tb]])
                    slot, bsl = w_get()
                    outs = fm_proj(slot, bsl, range(4), hr, b_h, 8,
                                   lambda k, cb, slot=slot: slot[:, k * 512 + cb * 128:k * 512 + (cb + 1) * 128])
                    for cb, (pt, bpt) in enumerate(outs):
                        dve.op(lambda e, pt=pt, cb=cb: e.tensor_scalar(out=qT[:, cb * T:(cb + 1) * T], in0=pt,
                                                                       scalar1=0.125, scalar2=None, op0=ALU.mult),
                               r=[bpt], w=[b_q[cb]])
                    slot, bsl = w_get()
                    outs = fm_proj(slot, bsl, range(4), hr, b_h, 8,
                                   lambda k, cb, slot=slot: slot[:, k * 512 + cb * 128:k * 512 + (cb + 1) * 128])
                    for cb, (pt, bpt) in enumerate(outs):
                        act.op(lambda e, pt=pt, cb=cb: e.activation(out=kT[:, cb, c0:c0 + T], in_=pt, func=AF.Copy),
                               r=[bpt], w=[b_kT[cb * NG + Q]])
                    slot, bsl = w_get()
                    for tb in range(4):
                        pt, bpt = ps_next()
                        for k in range(8):
                            pe.op(lambda e, pt=pt, k=k, tb=tb, slot=slot: e.matmul(
                                pt, lhsT=hT[:, k, tb * 128:(tb + 1) * 128], rhs=slot[:, k * 512:(k + 1) * 512],
                                start=(k == 0), stop=(k == 7)), r=[bsl, b_h[k]], w=[bpt], sig=(k == 7))
                        dve.op(lambda e, pt=pt, tb=tb: e.tensor_copy(out=vC[:, Q * 4 + tb, :], in_=pt),
                               r=[bpt], w=[b_vC[Q * 4 + tb]])
                    slot, bsl = w_get()
                    outs = fm_proj(slot, bsl, range(4), hr, b_h, 8,
                                   lambda k, cb, slot=slot: slot[:, k * 512 + cb * 128:k * 512 + (cb + 1) * 128])
                    for g, (pt, bpt) in enumerate(outs):
                        act.op(lambda e, pt=pt, g=g: e.activation(out=Pp[:, g * 528 + 16:(g + 1) * 528], in_=pt,
                                                                  func=AF.Copy), r=[bpt], w=[b_P[g]])
                        pool.op(lambda e, g=g: e.tensor_copy(out=Pp[:, g * 528:g * 528 + 16], in_=carry[:, g, :]),
                                r=[b_carry], w=[b_P[g]])

                    for g in range(4):
                        pt, bpt = ps_next()
                        lg = l * 4 + g
                        for tb in range(4):
                            pe.op(lambda e, pt=pt, tb=tb, lg=lg: e.matmul(
                                pt[:, tb * 128:(tb + 1) * 128], lhsT=onesrow[0:1, :],
                                rhs=bsrow[0:1, lg * 128:(lg + 1) * 128], start=(tb == 0), stop=False),
                                r=[b_const], w=[bpt], sig=False)
                        for tb in range(4):
                            pe.op(lambda e, pt=pt, tb=tb, g=g, lg=lg: e.matmul(
                                pt[:, tb * 128:(tb + 1) * 128],
                                lhsT=vln[:, tb * T + g * 128:tb * T + (g + 1) * 128], rhs=WT[:, lg, :],
                                start=False, stop=(tb == 3)), r=[b_vln[tb], b_const], w=[bpt], sig=(tb == 3))
                        dve.op(lambda e, pt=pt, g=g: e.tensor_tensor(out=uT[:, g * T:(g + 1) * T], in0=pt,
                                                                     in1=uT[:, g * T:(g + 1) * T], op=ALU.mult),
                               r=[bpt, b_u[g]], w=[b_u[g]])

                    for g in range(4):
                        base = g * 528
                        cur = Pp[:, base:base + 528]
                        curb = b_P[g]
                        sh = 1
                        lo = 0
                        for step in range(g + 1):
                            j = step % 2
                            dst = ptmp[:, j * 528:(j + 1) * 528]
                            lo += sh
                            pool.op(lambda e, dst=dst, cur=cur, sh=sh, lo=lo: e.tensor_tensor(
                                out=dst[:, lo:528], in0=cur[:, lo:528], in1=cur[:, lo - sh:528 - sh], op=ALU.add),
                                r=[curb], w=[b_ptmp[j]])
                            cur, curb = dst, b_ptmp[j]
                            sh *= 2
                        wsz = 2 ** (g + 1)
                        dve.op(lambda e, cur=cur, g=g, base=base, wsz=wsz: e.scalar_tensor_tensor(
                            out=dT[:, g, :], in0=cur[:, 16:528], scalar=1.0 / wsz, in1=Pp[:, base + 16:base + 528],
                            op0=ALU.mult, op1=ALU.subtract), r=[curb, b_P[g]], w=[b_d[g]])
                        if Q == 0:
                            j2 = (g + 1) % 2
                            t16 = ptmp[:, j2 * 528:j2 * 528 + 16]
                            dve.op(lambda e, cur=cur, g=g, t16=t16: e.tensor_tensor(
                                out=t16, in0=cur[:, 16:32], in1=prm[:, P_INVC + g * 16:P_INVC + (g + 1) * 16],
                                op=ALU.mult), r=[curb, b_const], w=[b_ptmp[j2]])
                            dve.op(lambda e, g=g, base=base, t16=t16: e.tensor_tensor(
                                out=dT[:, g, 0:16], in0=t16, in1=Pp[:, base + 16:base + 32], op=ALU.subtract),
                                r=[b_ptmp[j2], b_P[g]], w=[b_d[g]])
                    pool.op(lambda e: e.tensor_copy(out=carry[:],
                                                    in_=Pp.rearrange("p (g c) -> p g c", g=4)[:, :, 512:528]),
                            r=b_P, w=[b_carry])
                    for g in range(4):
                        pt, bpt = ps_next()
                        pe.op(lambda e, pt=pt, g=g: e.matmul(pt, lhsT=poolw[:, l * 4 + g, :], rhs=dT[:, g, :],
                                                            start=True, stop=True), r=[b_d[g], b_const], w=[bpt])
                        dve.op(lambda e, pt=pt, g=g: e.tensor_scalar(
                            out=dT[:, g, :], in0=pt, scalar1=prm[:, P_PSC + l * 4 + g:P_PSC + l * 4 + g + 1],
                            scalar2=None, op0=ALU.mult), r=[bpt, b_const], w=[b_d[g]])

                    tiles = []
                    for hp in range(4):
                        for e_ in range(2):
                            seq_i = list(range(4 * Q + 3, -1, -1))
                            for n_, i in enumerate(seq_i):
                                bdiag = i - 4 * Q
                                cc0 = 128 * bdiag if bdiag >= 0 else 0
                                tiles.append(dict(hp=hp, e=e_, i=i, b=bdiag, c0=cc0, first=(n_ == 0),
                                                  last=(n_ == len(seq_i) - 1), n=len(tiles)))
                    NT_ = len(tiles)

                    def st_Z(t):
                        n = t["n"]
                        zb = n % 4
                        Z = ps[:, zb, :]
                        pb = 64 * t["e"]
                        hp, i, cc0 = t["hp"], t["i"], t["c0"]
                        diag = t["b"] >= 0
                        pe.op(lambda e: e.matmul(Z[:, cc0:T], lhsT=kT[pb:pb + 64, hp, i * 128:(i + 1) * 128],
                                                 rhs=qT[pb:pb + 64, hp * T + cc0:(hp + 1) * T],
                                                 start=True, stop=not diag),
                              r=[b_kT[hp * NG + i // 4], b_q[hp]], w=[b_ps[zb]], sig=not diag)
                        if diag:
                            pe.op(lambda e: e.matmul(Z[:, cc0:cc0 + 128], lhsT=identb[:], rhs=negmask[:],
                                                     start=False, stop=True), r=[b_const], w=[b_ps[zb]])
                        j = n % 2
                        act.op(lambda e: e.activation(out=Ebuf[:, j * T + cc0:(j + 1) * T], in_=Z[:, cc0:T],
                                                      func=AF.Exp), r=[b_ps[zb]], w=[b_E[j]])

                    def st_S(t):
                        n = t["n"]
                        j = n % 2
                        cc0 = t["c0"]
                        act.op(lambda e: e.activation(out=Sbuf[:, j * T + cc0:(j + 1) * T],
                                                      in_=Ebuf[:, j * T + cc0:(j + 1) * T], func=AF.Ln, bias=1.0),
                               r=[b_E[j]], w=[b_S[j]])
                        if not t["last"]:
                            jn = (n + 1) % 2
                            c1 = cc0 + 128 if t["b"] >= 0 else cc0
                            if t["b"] >= 0:
                                dve.op(lambda e: e.tensor_copy(out=Lbuf[:, jn * T + cc0:jn * T + c1],
                                                               in_=Sbuf[:, j * T + cc0:j * T + c1]),
                                       r=[b_S[j]], w=[b_L[jn]])
                            if c1 < T and not t["first"]:
                                dve.op(lambda e: e.tensor_tensor(out=Lbuf[:, jn * T + c1:(jn + 1) * T],
                                                                 in0=Lbuf[:, j * T + c1:(j + 1) * T],
                                                                 in1=Sbuf[:, j * T + c1:(j + 1) * T], op=ALU.add),
                                       r=[b_S[j], b_L[j]], w=[b_L[jn]])

                    def st_C(t):
                        n = t["n"]
                        zb = n % 4
                        Z = ps[:, zb, :]
                        j = n % 2
                        cc0 = t["c0"]
                        c1 = cc0 + 128 if t["b"] >= 0 else cc0
                        has_l = (not t["first"]) and c1 < T
                        pe.op(lambda e: e.matmul(Z[:, cc0:T], lhsT=NTb[:], rhs=Sbuf[:, j * T + cc0:(j + 1) * T],
                                                 start=False, stop=not has_l, skip_group_check=True),
                              r=[b_S[j], b_const], w=[b_ps[zb]], sig=not has_l)
                        if has_l:
                            pe.op(lambda e: e.matmul(Z[:, c1:T], lhsT=negones[:], rhs=Lbuf[:, j * T + c1:(j + 1) * T],
                                                     start=False, stop=True, skip_group_check=True),
                                  r=[b_L[j], b_const], w=[b_ps[zb]])
                        act.op(lambda e: e.activation(out=Abuf[:, j * T + cc0:(j + 1) * T], in_=Z[:, cc0:T],
                                                      func=AF.Exp), r=[b_ps[zb]], w=[b_A[j]])

                    def st_V(t):
                        n = t["n"]
                        j = n % 2
                        cc0 = t["c0"]
                        hp, e_, i = t["hp"], t["e"], t["i"]
                        h_ = 2 * hp + e_
                        ob = 4 + hp % 2
                        O = ps[:, ob, :]
                        pb = 64 * e_
                        pe.op(lambda e: e.matmul(O[pb:pb + 64, cc0:T], lhsT=vC[:, i, h_ * 64:(h_ + 1) * 64],
                                                 rhs=Abuf[:, j * T + cc0:(j + 1) * T], start=t["first"],
                                                 stop=t["last"], skip_group_check=True),
                              r=[b_vC[i], b_A[j]], w=[b_ps[ob]])
                        if t["last"] and e_ == 1:
                            dve.op(lambda e: e.tensor_copy(out=qT[:, hp * T:(hp + 1) * T], in_=O),
                                   r=[b_ps[ob]], w=[b_q[hp]])

                    for n in range(-1, NT_ + 1):
                        if 0 <= n + 1 < NT_:
                            st_Z(tiles[n + 1])
                        if 0 <= n < NT_:
                            st_C(tiles[n])
                        if 0 <= n + 1 < NT_:
                            st_S(tiles[n + 1])
                        if 0 <= n - 1 < NT_:
                            st_V(tiles[n - 1])

                    if dbg and Q == NG - 1 and l == layers[-1] and b == nseq - 1:
                        sp.dma(lambda e: e.dma_start(out=dbg_u[:, :], in_=uT), ds_dbg[0], r=b_u)
                        sp.dma(lambda e: e.dma_start(out=dbg_q[:, :], in_=qT), ds_dbg[1], r=b_q)
                    br_rhs = [lambda kc: uT[:, kc * T:(kc + 1) * T], lambda kc: qT[:, kc * T:(kc + 1) * T],
                              lambda kc: dT[:, kc, :]]
                    br_buf = [b_u, b_q, b_d]
                    for dm in range(8):
                        if True:
                            slotG, bslG = w_get()
                            for nbr in range(3):
                                pg_, bpg = ps_next()
                                for k in range(8):
                                    pe.op(lambda e, pg_=pg_, k=k, nbr=nbr, slotG=slotG: e.matmul(
                                        pg_, lhsT=slotG[:, k * 384 + nbr * 128:k * 384 + (nbr + 1) * 128],
                                        rhs=hT[:, k, :], start=(k == 0), stop=(k == 7)),
                                        r=[bslG, b_h[k]], w=[bpg], sig=(k == 7))
                                pp_, bpp = ps_next()
                                for kc in range(4):
                                    pe.op(lambda e, pp_=pp_, kc=kc, nbr=nbr, slotG=slotG: e.matmul(
                                        pp_, lhsT=slotG[:, 3072 + kc * 384 + nbr * 128:3072 + kc * 384 + (nbr + 1) * 128],
                                        rhs=br_rhs[nbr](kc), start=(kc == 0), stop=(kc == 3)),
                                        r=[bslG, br_buf[nbr][kc]], w=[bpp], sig=(kc == 3))
                                j = nbr % 2
                                act.op(lambda e, pg_=pg_, j=j: e.activation(out=gsg[:, j * T:(j + 1) * T], in_=pg_,
                                                                            func=AF.Sigmoid),
                                       r=[bpg], w=[b_gsg[j]])
                                if nbr == 0:
                                    dve.op(lambda e, pp_=pp_, j=j: e.tensor_tensor(
                                        out=mrg_m, in0=pp_, in1=gsg[:, j * T:(j + 1) * T], op=ALU.mult),
                                        r=[bpp, b_gsg[j]], w=[b_mm])
                                else:
                                    dve.op(lambda e, pp_=pp_, j=j: e.tensor_tensor(
                                        out=mrg_t, in0=pp_, in1=gsg[:, j * T:(j + 1) * T], op=ALU.mult),
                                        r=[bpp, b_gsg[j]], w=[b_mt])
                                    if nbr == 1:
                                        pool.op(lambda e: e.tensor_tensor(out=mrg_m, in0=mrg_m, in1=mrg_t,
                                                                          op=ALU.add), r=[b_mm, b_mt], w=[b_mm])
                                    else:
                                        pool.op(lambda e, dm=dm: e.tensor_tensor(
                                            out=mrgT[:, dm * T:(dm + 1) * T], in0=mrg_m, in1=mrg_t, op=ALU.add),
                                            r=[b_mm, b_mt], w=[b_mrg[dm]])

                    if dbg and Q == NG - 1 and l == layers[-1] and b == nseq - 1:
                        sp.dma(lambda e: e.dma_start(out=dbg_m[:, :], in_=mrgT), ds_dbg[2], r=b_mrg)
                    for s_ in range(2):
                        slot, bsl = w_get()
                        outs = fm_proj(slot, bsl, range(4), lambda k: mrgT[:, k * T:(k + 1) * T], b_mrg, 8,
                                       lambda k, cb, slot=slot: slot[:, k * 512 + cb * 128:k * 512 + (cb + 1) * 128])
                        for cb, (pt, bpt) in enumerate(outs):
                            dm = s_ * 4 + cb
                            dve.op(lambda e, pt=pt, dm=dm: e.scalar_tensor_tensor(
                                out=xT[:, dm, c0:c0 + T], in0=pt, scalar=MB(16 + dm), in1=xT[:, dm, c0:c0 + T],
                                op0=ALU.mult, op1=ALU.add), r=[bpt, b_mod, b_xT[dm * NG + Q]], w=[b_xT[dm * NG + Q]])

                    norm(Q, lambda k: gs[:, 1, k:k + 1], lambda k: MB(24 + k),
                         lambda k: hT[:, k, :], b_h)

                    for jj in range(11):
                        slot, bsl = w_get()
                        for f2 in range(2):
                            jf = jj * 2 + f2
                            pgp, bpg = ps_next()
                            for k in range(8):
                                pe.op(lambda e, pgp=pgp, k=k, f2=f2, slot=slot: e.matmul(
                                    pgp, lhsT=slot[:, k * 512 + f2 * 128:k * 512 + (f2 + 1) * 128], rhs=hT[:, k, :],
                                    start=(k == 0), stop=(k == 7)), r=[bsl, b_h[k]], w=[bpg], sig=(k == 7))
                            pup, bpu = ps_next()
                            for k in range(8):
                                pe.op(lambda e, pup=pup, k=k, f2=f2, slot=slot: e.matmul(
                                    pup, lhsT=slot[:, k * 512 + 256 + f2 * 128:k * 512 + 256 + (f2 + 1) * 128],
                                    rhs=hT[:, k, :], start=(k == 0), stop=(k == 7)),
                                    r=[bsl, b_h[k]], w=[bpu], sig=(k == 7))
                            j = jf % 2
                            act.op(lambda e, pgp=pgp, j=j: e.activation(out=silb[:, j * T:(j + 1) * T], in_=pgp,
                                                                        func=AF.Silu), r=[bpg], w=[b_sil[j]])
                            dve.op(lambda e, pup=pup, j=j, jf=jf: e.tensor_tensor(
                                out=ffT[:, jf * T:(jf + 1) * T], in0=pup, in1=silb[:, j * T:(j + 1) * T],
                                op=ALU.mult), r=[bpu, b_sil[j]], w=[b_ff[jf]])
                    for dm in range(8):
                        slot, bsl = w_get()
                        pt, bpt = ps_next()
                        for jf in range(22):
                            pe.op(lambda e, pt=pt, jf=jf, slot=slot: e.matmul(
                                pt, lhsT=slot[:, jf * 128:(jf + 1) * 128], rhs=ffT[:, jf * T:(jf + 1) * T],
                                start=(jf == 0), stop=(jf == 21)), r=[bsl, b_ff[jf]], w=[bpt], sig=(jf == 21))
                        dve.op(lambda e, pt=pt, dm=dm: e.scalar_tensor_tensor(
                            out=xT[:, dm, c0:c0 + T], in0=pt, scalar=MB(40 + dm), in1=xT[:, dm, c0:c0 + T],
                            op0=ALU.mult, op1=ALU.add), r=[bpt, b_mod, b_xT[dm * NG + Q]], w=[b_xT[dm * NG + Q]])

            for Q in range(NG):
                if do_final:
                    norm(Q, lambda k: prm[:, P_FG + k:P_FG + k + 1], None,
                         lambda k: yT[:, k * T:(k + 1) * T], b_yT)
                    src = lambda k, tb: yT[:, k * T + tb * 128:k * T + (tb + 1) * 128]
                    srcb = lambda k: b_yT[k]
                else:
                    src = lambda k, tb, Q=Q: xT[:, k, Q * T + tb * 128:Q * T + (tb + 1) * 128]
                    srcb = lambda k, Q=Q: b_xT[k * NG + Q]
                for tb in range(4):
                    tg = Q * 4 + tb
                    j = tg % 2
                    for kh in range(2):
                        pt, bpt = ps_next()
                        for kk in range(4):
                            k = kh * 4 + kk
                            pe.op(lambda e, pt=pt, kk=kk, k=k, tb=tb, src=src: e.transpose(
                                pt[:, kk * 128:(kk + 1) * 128], src(k, tb), identf[:]),
                                r=[srcb(k), b_const], w=[bpt], sig=(kk == 3))
                        if kh == 0:
                            dve.op(lambda e, pt=pt, j=j: e.tensor_copy(out=xin[:, j * 1024:j * 1024 + 512], in_=pt),
                                   r=[bpt], w=[b_xin[j]])
                        else:
                            act.op(lambda e, pt=pt, j=j: e.activation(out=xin[:, j * 1024 + 512:(j + 1) * 1024],
                                                                      in_=pt, func=AF.Copy),
                                   r=[bpt], w=[b_xin[j]])
                    sp.dma(lambda e, tg=tg, j=j: e.dma_start(out=y_d[b, tg * 128:(tg + 1) * 128, :],
                                                             in_=xin[:, j * 1024:(j + 1) * 1024]),
                           ds_out[j], r=[b_xin[j]])
        sp.wait_for(b_xin)

        with nc.Block() as block:
            @block.tensor
            def _(e):
                pe.replay(e)

            @block.scalar
            def _(e):
                act.replay(e)

            @block.vector
            def _(e):
                dve.replay(e)

            @block.gpsimd
            def _(e):
                pool.replay(e)

            @block.sync
            def _(e):
                sp.replay(e)
    return nc


def _kp(w):
    K, C = w.shape
    return w.reshape(K // 128, 128, C).transpose(1, 0, 2)


def host_layout(inputs, layers=range(DEPTH)):
    w_in = inputs["w_in"]
    w_branch = inputs["w_branch"]
    w_out = inputs["w_out"]
    w_ffn_in = inputs["w_ffn_in"]
    w_ffn_out = inputs["w_ffn_out"]
    w_ada = inputs["w_ada"]
    wst = np.zeros((DEPTH, 128, WEL), np.float32)
    wada = np.zeros((DEPTH, 128, 8 * NMOD * D), np.float32)
    for l in layers:
        parts = []
        for s in range(6):
            parts.append(_kp(w_in[l][:, s * 512:(s + 1) * 512]).reshape(128, -1))
        for dm in range(8):
            blk = np.stack([_kp(w_in[l][:, 3072 + n * 1024 + dm * 128:3072 + n * 1024 + (dm + 1) * 128])
                            for n in range(3)], axis=2)
            parts.append(blk.reshape(128, -1))
            blk = np.stack([_kp(w_branch[l, n][:, dm * 128:(dm + 1) * 128]) for n in range(3)], axis=2)
            parts.append(blk.reshape(128, -1))
        for s in range(2):
            parts.append(_kp(w_out[l][:, s * 512:(s + 1) * 512]).reshape(128, -1))
        for jj in range(11):
            blk = np.stack([_kp(w_ffn_in[l][:, jj * 256:(jj + 1) * 256]),
                            _kp(w_ffn_in[l][:, DFF + jj * 256:DFF + (jj + 1) * 256])], axis=2)
            parts.append(blk.reshape(128, -1))
        for dm in range(8):
            parts.append(_kp(w_ffn_out[l][:, dm * 128:(dm + 1) * 128]).reshape(128, -1))
        wst[l] = np.concatenate(parts, axis=1)
        wa = _kp(w_ada[l])
        wada[l] = wa.reshape(128, 8, 12, 512).transpose(0, 2, 1, 3).reshape(128, -1)
    idx = np.arange(128)
    cm = np.zeros((128, 5, 128), np.float32)
    cm[:, 0] = np.eye(128, dtype=np.float32)
    cm[:, 1] = 1.0
    cm[:, 2] = np.where(idx[:, None] >= idx[None, :], -1.0, 0.0)
    cm[:, 3] = np.where(idx[:, None] < idx[None, :], 0.0, NEG)
    cm[:, 4] = np.where(idx[:, None] <= idx[None, :], 1.0, 0.0)
    cm = cm.reshape(128, 5 * 128)
    wsT = np.ascontiguousarray(inputs["gm_w_spatial"].transpose(3, 0, 1, 2)).reshape(128, -1)
    plw = np.ascontiguousarray(inputs["pool_w"].transpose(2, 0, 1, 3)).reshape(128, -1)
    bsr = np.ascontiguousarray(inputs["gm_b_spatial"]).reshape(1, -1)
    lnb = np.stack([np.broadcast_to(inputs["gm_ln_g"][:, None, :], (DEPTH, 128, 512)),
                    np.broadcast_to(inputs["gm_ln_b"][:, None, :], (DEPTH, 128, 512))], axis=2)
    lnb = np.ascontiguousarray(lnb).reshape(DEPTH, 128, 1024)
    prm = np.zeros((128, NPRM), np.float32)
    prm[:, P_G1:P_G1 + 32] = inputs["rms_g1"].reshape(DEPTH, 8, 128).transpose(2, 0, 1).reshape(128, 32)
    prm[:, P_G2:P_G2 + 32] = inputs["rms_g2"].reshape(DEPTH, 8, 128).transpose(2, 0, 1).reshape(128, 32)
    prm[:, P_FG:P_FG + 8] = inputs["final_g"].reshape(8, 128).T
    prm[:, P_BADA:P_BADA + 192] = inputs["b_ada"].reshape(DEPTH, 48, 128).transpose(2, 0, 1).reshape(128, 192)
    prm[:, P_PSC:P_PSC + 16] = inputs["pool_scale"].reshape(DEPTH, 4, 128).transpose(2, 0, 1).reshape(128, 16)
    pos = np.arange(16)
    invc = np.stack([1.0 / np.minimum(pos + 1, w) for w in (2, 4, 8, 16)]).astype(np.float32)
    prm[:, P_INVC:P_INVC + 64] = invc.reshape(1, 64)
    common = dict(wst=wst, wada=wada, cm=cm, wsT=wsT, plw=plw, bsr=bsr, lnb=lnb)
    return common, prm


def core_inputs(common, prm, x, c, b0, nseq):
    p = prm.copy()
    ct = np.zeros((128, 8, 4), np.float32)
    ct[:, :, :nseq] = c[b0:b0 + nseq].reshape(nseq, 8, 128).transpose(2, 1, 0)
    p[:, P_CT:P_CT + 32] = ct.reshape(128, 32)
    m = dict(common)
    m["prm"] = p
    m["x"] = np.ascontiguousarray(x[b0:b0 + nseq])
    return m


def kernel(**inputs):
    inputs = {k: np.asarray(v) for k, v in inputs.items()}
    x = inputs["x"]
    c = inputs["c"]
    Bt = x.shape[0]
    nseq = Bt // NCORES
    common, prm = host_layout(inputs)
    nc = build_program(nseq, list(range(DEPTH)), True)
    in_maps = [core_inputs(common, prm, x, c, i * nseq, nseq) for i in range(NCORES)]
    res = run_bass_kernel_spmd(nc, in_maps, core_ids=list(range(NCORES)))
    out = np.concatenate([r["y"] for r in res.results], axis=0)
    return out.astype(np.float32)
```

```python
import numpy as np
from contextlib import ExitStack
import concourse.bass as bass
import concourse.mybir as mybir
from concourse.bass_utils import run_bass_kernel_spmd

F32 = mybir.dt.float32
BF16 = mybir.dt.bfloat16
AF = mybir.ActivationFunctionType
ALU = mybir.AluOpType

D = 1024
SEQ = 2048
DEPTH = 4
BW = 512
DFF = 2816
NMOD = 6
T = 512
NG = SEQ // T
EPS = 1e-6
NCORES = 8
SLOT_EL = 4608
NSLOT = 3
NEG = -30000.0

SLABS = [4096] * 6 + [4608] * 8 + [4096] * 2 + [4096] * 11 + [2816] * 8
SLAB_OFF = np.concatenate([[0], np.cumsum(SLABS)]).astype(np.int64)
WEL = int(SLAB_OFF[-1])
NSLAB = len(SLABS)

P_G1 = 0
P_G2 = 32
P_FG = 64
P_BADA = 72
P_PSC = 264
P_CT = 280
P_INVC = 312
NPRM = 376


class Buf:
    __slots__ = ("name", "w", "r", "alias")

    def __init__(self, name):
        self.name = name
        self.w = None
        self.r = {}
        self.alias = []


def alias(la, lb):
    for a in la:
        for b in lb:
            a.alias.append(b)
            b.alias.append(a)


class Rec:
    def __init__(self):
        self.call = None

    def __getattr__(self, name):
        def f(*a, **k):
            self.call = (name, a, k)
            return self
        return f


def _rec(fn):
    r = Rec()
    fn(r)
    assert r.call is not None
    return r.call


class Eng:
    def __init__(self, name, sem):
        self.name = name
        self.sem = sem
        self.cnt = 0
        self.seen = {}
        self.prog = []

    def _need(self, waits, ev, same_ok):
        if ev is None:
            return
        key, val, semh = ev
        if same_ok and key == self.name:
            return
        if self.seen.get(key, 0) >= val:
            return
        self.seen[key] = val
        waits.append((semh, val))

    def _deps(self, r, w):
        waits = []
        for b in r:
            self._need(waits, b.w, False)
        same_ok = (self.name == "pe")
        for b in w:
            for bb in [b] + b.alias:
                self._need(waits, bb.w, same_ok)
                for key, (val, semh) in bb.r.items():
                    self._need(waits, (key, val, semh), same_ok)
        return waits

    def _commit(self, ev, r, w):
        for b in r:
            cur = b.r.get(ev[0])
            if cur is None or cur[0] < ev[1]:
                b.r[ev[0]] = (ev[1], ev[2])
        for b in w:
            b.w = ev
            b.r = {}
            for bb in b.alias:
                bb.r = {}

    def op(self, fn, r=(), w=(), sig=True):
        waits = self._deps(r, w)
        if sig:
            self.cnt += 1
            ev = (self.name, self.cnt, self.sem)
            self.prog.append((waits, _rec(fn), self.sem, 1))
        else:
            ev = (self.name, self.cnt + 1, self.sem)
            self.prog.append((waits, _rec(fn), None, 0))
        self._commit(ev, r, w)

    def dma(self, fn, dsem, r=(), w=()):
        waits = self._deps(r, w)
        dsem[2] += 16
        ev = (dsem[0], dsem[2], dsem[1])
        self.prog.append((waits, _rec(fn), dsem[1], 16))
        self._commit(ev, r, w)

    def wait_for(self, bufs):
        waits = []
        for b in bufs:
            self._need(waits, b.w, False)
            for key, (val, semh) in b.r.items():
                self._need(waits, (key, val, semh), False)
        self.prog.append((waits, None, None, 0))

    def replay(self, e):
        for waits, fn, sem, inc in self.prog:
            for semh, val in waits:
                e.wait_ge(semh, val)
            if fn is None:
                continue
            inst = getattr(e, fn[0])(*fn[1], **fn[2])
            if sem is not None:
                inst.then_inc(sem, inc)


def build_program(nseq, layers, do_final=True, dbg=False):
    nc = bass.Bass("TRN2", target_bir_lowering=False)
    x_d = nc.dram_tensor("x", [nseq, SEQ, D], F32, kind="ExternalInput").ap()
    y_d = nc.dram_tensor("y", [nseq, SEQ, D], F32, kind="ExternalOutput").ap()
    wst_d = nc.dram_tensor("wst", [DEPTH, 128, WEL], F32, kind="ExternalInput").ap()
    wada_d = nc.dram_tensor("wada", [DEPTH, 128, 8 * NMOD * D], F32, kind="ExternalInput").ap()
    prm_d = nc.dram_tensor("prm", [128, NPRM], F32, kind="ExternalInput").ap()
    cm_d = nc.dram_tensor("cm", [128, 5 * 128], F32, kind="ExternalInput").ap()
    wsT_d = nc.dram_tensor("wsT", [128, DEPTH * 4 * 128], F32, kind="ExternalInput").ap()
    plw_d = nc.dram_tensor("plw", [128, DEPTH * 4 * 128], F32, kind="ExternalInput").ap()
    bsr_d = nc.dram_tensor("bsr", [1, DEPTH * 4 * 128], F32, kind="ExternalInput").ap()
    lnb_d = nc.dram_tensor("lnb", [DEPTH, 128, 2 * 512], F32, kind="ExternalInput").ap()
    if dbg:
        dbg_u = nc.dram_tensor("dbg_u", [128, 4 * T], BF16, kind="ExternalOutput").ap()
        dbg_q = nc.dram_tensor("dbg_q", [128, 4 * T], BF16, kind="ExternalOutput").ap()
        dbg_m = nc.dram_tensor("dbg_m", [128, 8 * T], BF16, kind="ExternalOutput").ap()

    es = ExitStack()
    with es:
        def sb(name, shape, dt):
            return es.enter_context(nc.sbuf_tensor(name, shape, dt))

        def mksem(name):
            return es.enter_context(nc.semaphore(name))

        xT = sb("xT", [128, 8, SEQ], F32)
        kT = sb("kT", [128, 4, SEQ], BF16)
        vC = sb("vC", [128, 16, BW], BF16)
        slots = [sb(f"slot{i}", [128, SLOT_EL], BF16) for i in range(NSLOT)]
        identf = sb("identf", [128, 128], F32)
        identb = sb("identb", [128, 128], BF16)
        onesb = sb("onesb", [128, 128], BF16)
        negones = sb("negones", [128, 128], BF16)
        NTb = sb("NTb", [128, 128], BF16)
        negmask = sb("negmask", [128, 128], BF16)
        onesrow = sb("onesrow", [1, 128], BF16)
        bsrow = sb("bsrow", [1, DEPTH * 4 * 128], BF16)
        WT = sb("WT", [128, DEPTH * 4, 128], BF16)
        poolw = sb("poolw", [128, DEPTH * 4, 128], BF16)
        prm = sb("prm_sb", [128, NPRM], F32)
        mod = sb("mod", [128, DEPTH, 48, 4], F32)
        mh = sb("mh", [128, T], F32)
        lnb = sb("lnb_sb", [128, 2, 512], F32)
        cact = sb("cact", [128, 8, 4], BF16)
        gs = sb("gs", [128, 2, 8], F32)
        hT = sb("hT", [128, 8, T], BF16)
        dT = sb("dT", [128, 4, T], BF16)
        stats = sb("stats", [128, 4, 8], F32)
        carry = sb("carry", [128, 4, 16], F32)
        arX = sb("arX", [128, 26624], mybir.dt.uint8)
        arY = sb("arY", [128, 16896], mybir.dt.uint8)
        ps = es.enter_context(nc.psum_tensor("ps", [128, 8, 512], F32))

        def view(ar, off, n, dt):
            esz = 4 if dt == F32 else 2
            return ar[:, off:off + n * esz].bitcast(dt)

        uT = view(arX, 0, 4 * T, BF16)
        gv = view(arX, 4096, 2 * T, F32)
        vln = view(arX, 8192, 4 * T, BF16)
        qT = view(arX, 12288, 4 * T, BF16)
        att0 = 16384
        Ebuf = view(arX, att0, 2 * T, F32)
        Sbuf = view(arX, att0 + 4096, 2 * T, BF16)
        Abuf = view(arX, att0 + 6144, 2 * T, BF16)
        Lbuf = view(arX, att0 + 8192, 2 * T, BF16)
        sqb = view(arX, att0, 2 * T, BF16)
        rstd = view(arX, att0 + 2048, T, F32)
        tmpn = view(arX, att0 + 4096, 2 * T, F32)
        ffT = view(arX, 0, 22 * T, BF16)
        silb = view(arX, 22528, 2 * T, F32)
        yT = view(arX, 0, 8 * T, F32)
        cmf = view(arX, 0, 5 * 128, F32)
        wsTf = view(arX, 4096, DEPTH * 4 * 128, F32)
        plwf = view(arX, 4096 + 8192, DEPTH * 4 * 128, F32)
        Pp = view(arY, 0, 4 * 528, F32)
        ptmp = view(arY, 8448, 2 * 528, F32)
        gsg = view(arY, 0, 2 * T, F32)
        mrg_m = view(arY, 4096, T, F32)
        mrg_t = view(arY, 6144, T, F32)
        mrgT = view(arY, 8192, 8 * T, BF16)
        xin = view(arY, 0, 2 * 1024, F32)
        bsf = view(arY, 8192, DEPTH * 4 * 128, F32)

        B = {}

        def nb(name, n=None):
            if n is None:
                B[name] = Buf(name)
            else:
                B[name] = [Buf(f"{name}{i}") for i in range(n)]
            return B[name]

        b_xT = nb("xT", 8 * NG)
        b_kT = nb("kT", 4 * NG)
        b_vC = nb("vC", 16)
        b_slot = nb("slot", NSLOT)
        b_const = nb("const")
        b_mod = nb("mod")
        b_lnb = nb("lnb")
        b_gs = nb("gs")
        b_h = nb("h", 8)
        b_d = nb("d", 4)
        b_stats = nb("stats", 4)
        b_carry = nb("carry")
        b_ps = nb("ps", 8)
        b_u = nb("u", 4)
        b_gv = nb("gv", 2)
        b_vln = nb("vln", 4)
        b_q = nb("q", 4)
        b_E = nb("E", 2)
        b_S = nb("S", 2)
        b_A = nb("A", 2)
        b_L = nb("L", 2)
        b_sq = nb("sq", 2)
        b_rstd = nb("rstd")
        b_tmpn = nb("tmpn", 2)
        b_ff = nb("ff", 22)
        b_sil = nb("sil", 2)
        b_yT = nb("yT", 8)
        b_P = nb("P", 4)
        b_ptmp = nb("ptmp", 2)
        b_gsg = nb("gsg", 2)
        b_mm = nb("mm")
        b_mt = nb("mt")
        b_mrg = nb("mrg", 8)
        b_xin = nb("xin", 2)
        b_proX = nb("proX")
        b_proY = nb("proY")
        att_b = b_E + b_S + b_A + b_L
        nrm_b = b_sq + [b_rstd] + b_tmpn
        st1X = b_u + b_gv + b_vln + b_q + att_b + nrm_b
        st3X = b_ff + b_sil
        alias(att_b, nrm_b)
        alias(st1X, st3X)
        alias(st1X + st3X, b_yT + [b_proX])
        alias(b_yT, [b_proX])
        st1Y = b_P + b_ptmp
        st2Y = b_gsg + [b_mm, b_mt] + b_mrg
        alias(st1Y, st2Y)
        alias(st1Y + st2Y, b_xin + [b_proY])
        alias(b_xin, [b_proY])

        pe = Eng("pe", mksem("s_pe"))
        act = Eng("act", mksem("s_act"))
        dve = Eng("dve", mksem("s_dve"))
        pool = Eng("pool", mksem("s_pool"))
        sp = Eng("sp", mksem("s_sp"))
        ds_slot = [[f"dslot{i}", mksem(f"d_slot{i}"), 0] for i in range(NSLOT)]
        ds_c = ["dc", mksem("d_c"), 0]
        ds_xin = [[f"dxin{i}", mksem(f"d_xin{i}"), 0] for i in range(2)]
        ds_out = [[f"dout{i}", mksem(f"d_out{i}"), 0] for i in range(2)]
        ds_lnb = ["dlnb", mksem("d_lnb"), 0]
        ds_dbg = [[f"ddbg{i}", mksem(f"d_dbg{i}"), 0] for i in range(3)]

        wplan = []
        for l in layers:
            for s_ in range(12):
                wplan.append((lambda l=l, s_=s_: wada_d[l, :, s_ * 4096:(s_ + 1) * 4096], 4096))
        for b in range(nseq):
            for l in layers:
                for Q in range(NG):
                    for s_ in range(NSLAB):
                        o0, o1 = int(SLAB_OFF[s_]), int(SLAB_OFF[s_ + 1])
                        wplan.append((lambda l=l, o0=o0, o1=o1: wst_d[l, :, o0:o1], o1 - o0))
        wstate = {"issued": 0, "next": 0}

        def w_issue():
            i = wstate["issued"]
            apf, nel = wplan[i]
            si = i % NSLOT
            src = apf()
            dst = slots[si][:, 0:nel]
            pool.dma(lambda e, dst=dst, src=src: e.dma_start(out=dst, in_=src), ds_slot[si], r=(), w=[b_slot[si]])
            wstate["issued"] = i + 1

        def w_get():
            i = wstate["next"]
            while wstate["issued"] < min(i + NSLOT, len(wplan)):
                w_issue()
            wstate["next"] = i + 1
            si = i % NSLOT
            return slots[si], b_slot[si]

        psrot = {"i": 0}

        def ps_next(lo=0, hi=8):
            i = psrot["i"]
            psrot["i"] = i + 1
            k = lo + i % (hi - lo)
            return ps[:, k, :], b_ps[k]

        sp.dma(lambda e: e.dma_start(out=prm[:], in_=prm_d[:, :]), ds_c, w=[b_const])
        sp.dma(lambda e: e.dma_start(out=cmf, in_=cm_d[:, :]), ds_c, w=[b_proX])
        sp.dma(lambda e: e.dma_start(out=wsTf, in_=wsT_d[:, :]), ds_c, w=[b_proX])
        sp.dma(lambda e: e.dma_start(out=plwf, in_=plw_d[:, :]), ds_c, w=[b_proX])
        sp.dma(lambda e: e.dma_start(out=bsf[0:1, :], in_=bsr_d[:, :]), ds_c, w=[b_proY])
        b_const.w = b_proX.w = b_proY.w
        cw = [b_const]
        dve.op(lambda e: e.tensor_copy(out=identf[:], in_=cmf[:, 0:128]), r=[b_proX], w=cw)
        dve.op(lambda e: e.tensor_copy(out=identb[:], in_=cmf[:, 0:128]), r=[b_proX], w=cw)
        dve.op(lambda e: e.tensor_copy(out=onesb[:], in_=cmf[:, 128:256]), r=[b_proX], w=cw)
        dve.op(lambda e: e.tensor_copy(out=NTb[:], in_=cmf[:, 256:384]), r=[b_proX], w=cw)
        dve.op(lambda e: e.tensor_copy(out=negmask[:], in_=cmf[:, 384:512]), r=[b_proX], w=cw)
        dve.op(lambda e: e.tensor_scalar(out=negones[:], in0=cmf[:, 128:256], scalar1=-1.0, scalar2=None,
                                         op0=ALU.mult), r=[b_proX], w=cw)
        dve.op(lambda e: e.tensor_copy(out=onesrow[:], in_=cmf[0:1, 128:256]), r=[b_proX], w=cw)
        for lg in range(DEPTH * 4):
            dve.op(lambda e, lg=lg: e.tensor_tensor(out=WT[:, lg, :], in0=wsTf[:, lg * 128:(lg + 1) * 128],
                                                    in1=cmf[:, 512:640], op=ALU.mult), r=[b_proX], w=cw)
        dve.op(lambda e: e.tensor_copy(out=poolw[:].rearrange("p a b -> p (a b)"), in_=plwf), r=[b_proX], w=cw)
        dve.op(lambda e: e.tensor_copy(out=bsrow[:], in_=bsf[0:1, :]), r=[b_proY], w=cw)
        pool.op(lambda e: e.memset(mh[:], -0.5), w=cw)
        pool.op(lambda e: e.memset(carry[:], 0.0), w=[b_carry])
        act.op(lambda e: e.activation(out=cact[:].rearrange("p a b -> p (a b)"), in_=prm[:, P_CT:P_CT + 32],
                                      func=AF.Silu), r=[b_const], w=cw)
        for li, l in enumerate(layers):
            for s_ in range(12):
                slot, bsl = w_get()
                for cb in range(4):
                    pt, bpt = ps_next()
                    for k in range(8):
                        pe.op(lambda e, pt=pt, slot=slot, k=k, cb=cb: e.matmul(
                            pt[:, 0:4], lhsT=slot[:, k * 512 + cb * 128:k * 512 + (cb + 1) * 128],
                            rhs=cact[:, k, :], start=(k == 0), stop=(k == 7)),
                            r=[bsl, b_const], w=[bpt], sig=(k == 7))
                    col = s_ * 4 + cb
                    dve.op(lambda e, pt=pt, l=l, col=col: e.tensor_scalar(
                        out=mod[:, l, col, :], in0=pt[:, 0:4],
                        scalar1=prm[:, P_BADA + l * 48 + col:P_BADA + l * 48 + col + 1], scalar2=None,
                        op0=ALU.add), r=[bpt, b_const], w=[b_mod])

        def norm(Q, gvec, shvec, out_fn, out_bufs, xsrc=None):
            c0 = Q * T
            pt, bpt = ps_next()
            for k in range(8):
                j = k % 2
                pool.op(lambda e, k=k, j=j: e.tensor_tensor(out=sqb[:, j * T:(j + 1) * T], in0=xT[:, k, c0:c0 + T],
                                                            in1=xT[:, k, c0:c0 + T], op=ALU.mult),
                        r=[b_xT[k * NG + Q]], w=[b_sq[j]])
                pe.op(lambda e, k=k, j=j, pt=pt: e.matmul(pt, lhsT=onesb[:], rhs=sqb[:, j * T:(j + 1) * T],
                                                         start=(k == 0), stop=(k == 7)),
                      r=[b_sq[j], b_const], w=[bpt], sig=True)
            act.op(lambda e, pt=pt: e.activation(out=tmpn[:, 0:T], in_=pt, func=AF.Ln, scale=1.0 / D, bias=EPS),
                   r=[bpt], w=[b_tmpn[0]])
            act.op(lambda e: e.activation(out=rstd, in_=tmpn[:, 0:T], func=AF.Exp, scale=-0.5),
                   r=[b_tmpn[0]], w=[b_rstd])
            for k in range(8):
                j = k % 2
                if shvec is None:
                    dve.op(lambda e, k=k: e.scalar_tensor_tensor(
                        out=out_fn(k), in0=xT[:, k, c0:c0 + T], scalar=gvec(k), in1=rstd,
                        op0=ALU.mult, op1=ALU.mult), r=[b_xT[k * NG + Q], b_rstd, b_const], w=[out_bufs[k]])
                else:
                    dve.op(lambda e, k=k, j=j: e.tensor_tensor(
                        out=tmpn[:, j * T:(j + 1) * T], in0=xT[:, k, c0:c0 + T], in1=rstd, op=ALU.mult),
                        r=[b_xT[k * NG + Q], b_rstd], w=[b_tmpn[j]])
                    act.op(lambda e, k=k, j=j: e.activation(
                        out=out_fn(k), in_=tmpn[:, j * T:(j + 1) * T], func=AF.Identity, scale=gvec(k),
                        bias=shvec(k)), r=[b_tmpn[j], b_mod, b_gs], w=[out_bufs[k]])

        def fm_proj(slot, bsl, cbs, rhs_fn, rhs_bufs, nk, wcol_fn):
            outs = []
            for cb in cbs:
                pt, bpt = ps_next()
                for k in range(nk):
                    pe.op(lambda e, pt=pt, k=k, cb=cb: e.matmul(pt, lhsT=wcol_fn(k, cb), rhs=rhs_fn(k),
                                                               start=(k == 0), stop=(k == nk - 1)),
                          r=[bsl, rhs_bufs[k]], w=[bpt], sig=(k == nk - 1))
                outs.append((pt, bpt))
            return outs

        for b in range(nseq):
            for tb in range(16):
                j = tb % 2
                sp.dma(lambda e, tb=tb, j=j: e.dma_start(out=xin[:, j * 1024:(j + 1) * 1024],
                                                         in_=x_d[b, tb * 128:(tb + 1) * 128, :]),
                       ds_xin[j], w=[b_xin[j]])
                for kh in range(2):
                    pt, bpt = ps_next()
                    for kk in range(4):
                        k = kh * 4 + kk
                        pe.op(lambda e, pt=pt, kk=kk, k=k, j=j: e.transpose(
                            pt[:, kk * 128:(kk + 1) * 128], xin[:, j * 1024 + k * 128:j * 1024 + (k + 1) * 128],
                            identf[:]), r=[b_xin[j], b_const], w=[bpt], sig=(kk == 3))
                    Qx = tb // 4
                    wl = [b_xT[(kh * 4 + kk) * NG + Qx] for kk in range(4)]
                    eng = dve if kh == 0 else act
                    if kh == 0:
                        dve.op(lambda e, pt=pt, kh=kh, tb=tb: e.tensor_copy(
                            out=xT[:, kh * 4:(kh + 1) * 4, tb * 128:(tb + 1) * 128],
                            in_=pt.rearrange("p (a b) -> p a b", a=4)), r=[bpt], w=wl)
                    else:
                        act.op(lambda e, pt=pt, kh=kh, tb=tb: e.activation(
                            out=xT[:, kh * 4:(kh + 1) * 4, tb * 128:(tb + 1) * 128],
                            in_=pt.rearrange("p (a b) -> p a b", a=4), func=AF.Copy), r=[bpt], w=wl)

            for l in layers:
                MB = lambda c: mod[:, l, c, b:b + 1]
                for t_, (pg, sc0) in enumerate(((P_G1, 8), (P_G2, 32))):
                    dve.op(lambda e, t_=t_, pg=pg, sc0=sc0: e.scalar_tensor_tensor(
                        out=gs[:, t_, :], in0=mod[:, l, sc0:sc0 + 8, b], scalar=1.0,
                        in1=prm[:, pg + l * 8:pg + l * 8 + 8], op0=ALU.add, op1=ALU.mult),
                        r=[b_mod, b_const], w=[b_gs])
                sp.dma(lambda e: e.dma_start(out=lnb[:].rearrange("p a b -> p (a b)"), in_=lnb_d[l, :, :]),
                       ds_lnb, w=[b_lnb])
                pool.op(lambda e: e.memset(carry[:], 0.0), w=[b_carry])

                for Q in range(NG):
                    c0 = Q * T
                    norm(Q, lambda k: gs[:, 0, k:k + 1], lambda k: MB(0 + k),
                         lambda k: hT[:, k, :], b_h)
                    hr = lambda k: hT[:, k, :]

                    slot, bsl = w_get()
                    outs = fm_proj(slot, bsl, range(4), hr, b_h, 8,
                                   lambda k, cb, slot=slot: slot[:, k * 512 + cb * 128:k * 512 + (cb + 1) * 128])
                    for cb, (pt, bpt) in enumerate(outs):
                        act.op(lambda e, pt=pt, cb=cb: e.activation(out=uT[:, cb * T:(cb + 1) * T], in_=pt,
                                                                    func=AF.Gelu_apprx_tanh),
                               r=[bpt], w=[b_u[cb]])
                    slot, bsl = w_get()
                    for tb in range(4):
                        pt, bpt = ps_next()
                        j = tb % 2
                        for k in range(8):
                            pe.op(lambda e, pt=pt, k=k, tb=tb, slot=slot: e.matmul(
                                pt, lhsT=hT[:, k, tb * 128:(tb + 1) * 128], rhs=slot[:, k * 512:(k + 1) * 512],
                                start=(k == 0), stop=(k == 7)), r=[bsl, b_h[k]], w=[bpt], sig=(k == 7))
                        gvj = gv[:, j * T:(j + 1) * T]
                        act.op(lambda e, pt=pt, gvj=gvj: e.activation(out=gvj, in_=pt, func=AF.Gelu_apprx_tanh),
                               r=[bpt], w=[b_gv[j]])
                        dve.op(lambda e, gvj=gvj, tb=tb: e.bn_stats(out=stats[:, tb, 0:6], in_=gvj),
                               r=[b_gv[j]], w=[b_stats[tb]])
                        dve.op(lambda e, tb=tb: e.bn_aggr(out=stats[:, tb, 6:8], in_=stats[:, tb, 0:6]),
                               r=[b_stats[tb]], w=[b_stats[tb]])
                        dve.op(lambda e, tb=tb: e.tensor_scalar(out=stats[:, tb, 7:8], in0=stats[:, tb, 7:8],
                                                                scalar1=EPS, scalar2=None, op0=ALU.add),
                               r=[b_stats[tb]], w=[b_stats[tb]])
                        pool.op(lambda e, tb=tb: e.tensor_tensor(out=stats[:, tb, 7:8], in0=stats[:, tb, 7:8],
                                                                 in1=mh[:, 0:1], op=ALU.pow),
                                r=[b_stats[tb], b_const], w=[b_stats[tb]])
                        dve.op(lambda e, gvj=gvj, tb=tb: e.tensor_scalar(
                            out=gvj, in0=gvj, scalar1=stats[:, tb, 6:7], scalar2=stats[:, tb, 7:8],
                            op0=ALU.subtract, op1=ALU.mult), r=[b_gv[j], b_stats[tb]], w=[b_gv[j]])
                        dve.op(lambda e, gvj=gvj: e.tensor_tensor(out=gvj, in0=gvj, in1=lnb[:, 0, :], op=ALU.mult),
                               r=[b_gv[j], b_lnb], w=[b_gv[j]])
                        pool.op(lambda e, gvj=gvj, tb=tb: e.tensor_tensor(
                            out=vln[:, tb * T:(tb + 1) * T], in0=gvj, in1=lnb[:, 1, :], op=ALU.add),
                            r=[b_gv[j], b_lnb], w=[b_vln[tb]])
                    slot, bsl = w_get()
                    outs = fm_proj(slot, bsl, range(4), hr, b_h, 8,
                                   lambda k, cb, slot=slot: slot[:, k * 512 + cb * 128:k * 512 + (cb + 1) * 128])
                    for cb, (pt, bpt) in enumerate(outs):
                        dve.op(lambda e, pt=pt, cb=cb: e.tensor_scalar(out=qT[:, cb * T:(cb + 1) * T], in0=pt,
                                                                       scalar1=0.125, scalar2=None, op0=ALU.mult),
                               r=[bpt], w=[b_q[cb]])
                    slot, bsl = w_get()
                    outs = fm_proj(slot, bsl, range(4), hr, b_h, 8,
                                   lambda k, cb, slot=slot: slot[:, k * 512 + cb * 128:k * 512 + (cb + 1) * 128])
                    for cb, (pt, bpt) in enumerate(outs):
                        act.op(lambda e, pt=pt, cb=cb: e.activation(out=kT[:, cb, c0:c0 + T], in_=pt, func=AF.Copy),
                               r=[bpt], w=[b_kT[cb * NG + Q]])
                    slot, bsl = w_get()
                    for tb in range(4):
                        pt, bpt = ps_next()
                        for k in range(8):
                            pe.op(lambda e, pt=pt, k=k, tb=tb, slot=slot: e.matmul(
                                pt, lhsT=hT[:, k, tb * 128:(tb + 1) * 128], rhs=slot[:, k * 512:(k + 1) * 512],
                                start=(k == 0), stop=(k == 7)), r=[bsl, b_h[k]], w=[bpt], sig=(k == 7))
                        dve.op(lambda e, pt=pt, tb=tb: e.tensor_copy(out=vC[:, Q * 4 + tb, :], in_=pt),
                               r=[bpt], w=[b_vC[Q * 4 + tb]])
                    slot, bsl = w_get()
                    outs = fm_proj(slot, bsl, range(4), hr, b_h, 8,
                                   lambda k, cb, slot=slot: slot[:, k * 512 + cb * 128:k * 512 + (cb + 1) * 128])
                    for g, (pt, bpt) in enumerate(outs):
                        act.op(lambda e, pt=pt, g=g: e.activation(out=Pp[:, g * 528 + 16:(g + 1) * 528], in_=pt,
                                                                  func=AF.Copy), r=[bpt], w=[b_P[g]])
                        pool.op(lambda e, g=g: e.tensor_copy(out=Pp[:, g * 528:g * 528 + 16], in_=carry[:, g, :]),
                                r=[b_carry], w=[b_P[g]])

                    for g in range(4):
                        pt, bpt = ps_next()
                        lg = l * 4 + g
                        for tb in range(4):
                            pe.op(lambda e, pt=pt, tb=tb, lg=lg: e.matmul(
                                pt[:, tb * 128:(tb + 1) * 128], lhsT=onesrow[0:1, :],
                                rhs=bsrow[0:1, lg * 128:(lg + 1) * 128], start=(tb == 0), stop=False),
                                r=[b_const], w=[bpt], sig=False)
                        for tb in range(4):
                            pe.op(lambda e, pt=pt, tb=tb, g=g, lg=lg: e.matmul(
                                pt[:, tb * 128:(tb + 1) * 128],
                                lhsT=vln[:, tb * T + g * 128:tb * T + (g + 1) * 128], rhs=WT[:, lg, :],
                                start=False, stop=(tb == 3)), r=[b_vln[tb], b_const], w=[bpt], sig=(tb == 3))
                        dve.op(lambda e, pt=pt, g=g: e.tensor_tensor(out=uT[:, g * T:(g + 1) * T], in0=pt,
                                                                     in1=uT[:, g * T:(g + 1) * T], op=ALU.mult),
                               r=[bpt, b_u[g]], w=[b_u[g]])

                    for g in range(4):
                        base = g * 528
                        cur = Pp[:, base:base + 528]
                        curb = b_P[g]
                        sh = 1
                        lo = 0
                        for step in range(g + 1):
                            j = step % 2
                            dst = ptmp[:, j * 528:(j + 1) * 528]
                            lo += sh
                            pool.op(lambda e, dst=dst, cur=cur, sh=sh, lo=lo: e.tensor_tensor(
                                out=dst[:, lo:528], in0=cur[:, lo:528], in1=cur[:, lo - sh:528 - sh], op=ALU.add),
                                r=[curb], w=[b_ptmp[j]])
                            cur, curb = dst, b_ptmp[j]
                            sh *= 2
                        wsz = 2 ** (g + 1)
                        dve.op(lambda e, cur=cur, g=g, base=base, wsz=wsz: e.scalar_tensor_tensor(
                            out=dT[:, g, :], in0=cur[:, 16:528], scalar=1.0 / wsz, in1=Pp[:, base + 16:base + 528],
                            op0=ALU.mult, op1=ALU.subtract), r=[curb, b_P[g]], w=[b_d[g]])
                        if Q == 0:
                            j2 = (g + 1) % 2
                            t16 = ptmp[:, j2 * 528:j2 * 528 + 16]
                            dve.op(lambda e, cur=cur, g=g, t16=t16: e.tensor_tensor(
                                out=t16, in0=cur[:, 16:32], in1=prm[:, P_INVC + g * 16:P_INVC + (g + 1) * 16],
                                op=ALU.mult), r=[curb, b_const], w=[b_ptmp[j2]])
                            dve.op(lambda e, g=g, base=base, t16=t16: e.tensor_tensor(
                                out=dT[:, g, 0:16], in0=t16, in1=Pp[:, base + 16:base + 32], op=ALU.subtract),
                                r=[b_ptmp[j2], b_P[g]], w=[b_d[g]])
                    pool.op(lambda e: e.tensor_copy(out=carry[:],
                                                    in_=Pp.rearrange("p (g c) -> p g c", g=4)[:, :, 512:528]),
                            r=b_P, w=[b_carry])
                    for g in range(4):
                        pt, bpt = ps_next()
                        pe.op(lambda e, pt=pt, g=g: e.matmul(pt, lhsT=poolw[:, l * 4 + g, :], rhs=dT[:, g, :],
                                                            start=True, stop=True), r=[b_d[g], b_const], w=[bpt])
                        dve.op(lambda e, pt=pt, g=g: e.tensor_scalar(
                            out=dT[:, g, :], in0=pt, scalar1=prm[:, P_PSC + l * 4 + g:P_PSC + l * 4 + g + 1],
                            scalar2=None, op0=ALU.mult), r=[bpt, b_const], w=[b_d[g]])

                    tiles = []
                    for hp in range(4):
                        for e_ in range(2):
                            seq_i = list(range(4 * Q + 3, -1, -1))
                            for n_, i in enumerate(seq_i):
                                bdiag = i - 4 * Q
                                cc0 = 128 * bdiag if bdiag >= 0 else 0
                                tiles.append(dict(hp=hp, e=e_, i=i, b=bdiag, c0=cc0, first=(n_ == 0),
                                                  last=(n_ == len(seq_i) - 1), n=len(tiles)))
                    NT_ = len(tiles)

                    def st_Z(t):
                        n = t["n"]
                        zb = n % 4
                        Z = ps[:, zb, :]
                        pb = 64 * t["e"]
                        hp, i, cc0 = t["hp"], t["i"], t["c0"]
                        diag = t["b"] >= 0
                        pe.op(lambda e: e.matmul(Z[:, cc0:T], lhsT=kT[pb:pb + 64, hp, i * 128:(i + 1) * 128],
                                                 rhs=qT[pb:pb + 64, hp * T + cc0:(hp + 1) * T],
                                                 start=True, stop=not diag),
                              r=[b_kT[hp * NG + i // 4], b_q[hp]], w=[b_ps[zb]], sig=not diag)
                        if diag:
                            pe.op(lambda e: e.matmul(Z[:, cc0:cc0 + 128], lhsT=identb[:], rhs=negmask[:],
                                                     start=False, stop=True), r=[b_const], w=[b_ps[zb]])
                        j = n % 2
                        act.op(lambda e: e.activation(out=Ebuf[:, j * T + cc0:(j + 1) * T], in_=Z[:, cc0:T],
                                                      func=AF.Exp), r=[b_ps[zb]], w=[b_E[j]])

                    def st_S(t):
                        n = t["n"]
                        j = n % 2
                        cc0 = t["c0"]
                        act.op(lambda e: e.activation(out=Sbuf[:, j * T + cc0:(j + 1) * T],
                                                      in_=Ebuf[:, j * T + cc0:(j + 1) * T], func=AF.Ln, bias=1.0),
                               r=[b_E[j]], w=[b_S[j]])
                        if not t["last"]:
                            jn = (n + 1) % 2
                            c1 = cc0 + 128 if t["b"] >= 0 else cc0
                            if t["b"] >= 0:
                                dve.op(lambda e: e.tensor_copy(out=Lbuf[:, jn * T + cc0:jn * T + c1],
                                                               in_=Sbuf[:, j * T + cc0:j * T + c1]),
                                       r=[b_S[j]], w=[b_L[jn]])
                            if c1 < T and not t["first"]:
                                dve.op(lambda e: e.tensor_tensor(out=Lbuf[:, jn * T + c1:(jn + 1) * T],
                                                                 in0=Lbuf[:, j * T + c1:(j + 1) * T],
                                                                 in1=Sbuf[:, j * T + c1:(j + 1) * T], op=ALU.add),
                                       r=[b_S[j], b_L[j]], w=[b_L[jn]])

                    def st_C(t):
                        n = t["n"]
                        zb = n % 4
                        Z = ps[:, zb, :]
                        j = n % 2
                        cc0 = t["c0"]
                        c1 = cc0 + 128 if t["b"] >= 0 else cc0
                        has_l = (not t["first"]) and c1 < T
                        pe.op(lambda e: e.matmul(Z[:, cc0:T], lhsT=NTb[:], rhs=Sbuf[:, j * T + cc0:(j + 1) * T],
                                                 start=False, stop=not has_l, skip_group_check=True),
                              r=[b_S[j], b_const], w=[b_ps[zb]], sig=not has_l)
                        if has_l:
                            pe.op(lambda e: e.matmul(Z[:, c1:T], lhsT=negones[:], rhs=Lbuf[:, j * T + c1:(j + 1) * T],
                                                     start=False, stop=True, skip_group_check=True),
                                  r=[b_L[j], b_const], w=[b_ps[zb]])
                        act.op(lambda e: e.activation(out=Abuf[:, j * T + cc0:(j + 1) * T], in_=Z[:, cc0:T],
                                                      func=AF.Exp), r=[b_ps[zb]], w=[b_A[j]])

                    def st_V(t):
                        n = t["n"]
                        j = n % 2
                        cc0 = t["c0"]
                        hp, e_, i = t["hp"], t["e"], t["i"]
                        h_ = 2 * hp + e_
                        ob = 4 + hp % 2
                        O = ps[:, ob, :]
                        pb = 64 * e_
                        pe.op(lambda e: e.matmul(O[pb:pb + 64, cc0:T], lhsT=vC[:, i, h_ * 64:(h_ + 1) * 64],
                                                 rhs=Abuf[:, j * T + cc0:(j + 1) * T], start=t["first"],
                                                 stop=t["last"], skip_group_check=True),
                              r=[b_vC[i], b_A[j]], w=[b_ps[ob]])
                        if t["last"] and e_ == 1:
                            dve.op(lambda e: e.tensor_copy(out=qT[:, hp * T:(hp + 1) * T], in_=O),
                                   r=[b_ps[ob]], w=[b_q[hp]])

                    for n in range(-1, NT_ + 1):
                        if 0 <= n + 1 < NT_:
                            st_Z(tiles[n + 1])
                        if 0 <= n < NT_:
                            st_C(tiles[n])
                        if 0 <= n + 1 < NT_:
                            st_S(tiles[n + 1])
                        if 0 <= n - 1 < NT_:
                            st_V(tiles[n - 1])

                    if dbg and Q == NG - 1 and l == layers[-1] and b == nseq - 1:
                        sp.dma(lambda e: e.dma_start(out=dbg_u[:, :], in_=uT), ds_dbg[0], r=b_u)
                        sp.dma(lambda e: e.dma_start(out=dbg_q[:, :], in_=qT), ds_dbg[1], r=b_q)
                    br_rhs = [lambda kc: uT[:, kc * T:(kc + 1) * T], lambda kc: qT[:, kc * T:(kc + 1) * T],
                              lambda kc: dT[:, kc, :]]
                    br_buf = [b_u, b_q, b_d]
                    for dm in range(8):
                        if True:
                            slotG, bslG = w_get()
                            for nbr in range(3):
                                pg_, bpg = ps_next()
                                for k in range(8):
                                    pe.op(lambda e, pg_=pg_, k=k, nbr=nbr, slotG=slotG: e.matmul(
                                        pg_, lhsT=slotG[:, k * 384 + nbr * 128:k * 384 + (nbr + 1) * 128],
                                        rhs=hT[:, k, :], start=(k == 0), stop=(k == 7)),
                                        r=[bslG, b_h[k]], w=[bpg], sig=(k == 7))
                                pp_, bpp = ps_next()
                                for kc in range(4):
                                    pe.op(lambda e, pp_=pp_, kc=kc, nbr=nbr, slotG=slotG: e.matmul(
                                        pp_, lhsT=slotG[:, 3072 + kc * 384 + nbr * 128:3072 + kc * 384 + (nbr + 1) * 128],
                                        rhs=br_rhs[nbr](kc), start=(kc == 0), stop=(kc == 3)),
                                        r=[bslG, br_buf[nbr][kc]], w=[bpp], sig=(kc == 3))
                                j = nbr % 2
                                act.op(lambda e, pg_=pg_, j=j: e.activation(out=gsg[:, j * T:(j + 1) * T], in_=pg_,
                                                                            func=AF.Sigmoid),
                                       r=[bpg], w=[b_gsg[j]])
                                if nbr == 0:
                                    dve.op(lambda e, pp_=pp_, j=j: e.tensor_tensor(
                                        out=mrg_m, in0=pp_, in1=gsg[:, j * T:(j + 1) * T], op=ALU.mult),
                                        r=[bpp, b_gsg[j]], w=[b_mm])
                                else:
                                    dve.op(lambda e, pp_=pp_, j=j: e.tensor_tensor(
                                        out=mrg_t, in0=pp_, in1=gsg[:, j * T:(j + 1) * T], op=ALU.mult),
                                        r=[bpp, b_gsg[j]], w=[b_mt])
                                    if nbr == 1:
                                        pool.op(lambda e: e.tensor_tensor(out=mrg_m, in0=mrg_m, in1=mrg_t,
                                                                          op=ALU.add), r=[b_mm, b_mt], w=[b_mm])
                                    else:
                                        pool.op(lambda e, dm=dm: e.tensor_tensor(
                                            out=mrgT[:, dm * T:(dm + 1) * T], in0=mrg_m, in1=mrg_t, op=ALU.add),
                                            r=[b_mm, b_mt], w=[b_mrg[dm]])

                    if dbg and Q == NG - 1 and l == layers[-1] and b == nseq - 1:
                        sp.dma(lambda e: e.dma_start(out=dbg_m[:, :], in_=mrgT), ds_dbg[2], r=b_mrg)
                    for s_ in range(2):
                        slot, bsl = w_get()
                        outs = fm_proj(slot, bsl, range(4), lambda k: mrgT[:, k * T:(k + 1) * T], b_mrg, 8,
                                       lambda k, cb, slot=slot: slot[:, k * 512 + cb * 128:k * 512 + (cb + 1) * 128])
                        for cb, (pt, bpt) in enumerate(outs):
                            dm = s_ * 4 + cb
                            dve.op(lambda e, pt=pt, dm=dm: e.scalar_tensor_tensor(
                                out=xT[:, dm, c0:c0 + T], in0=pt, scalar=MB(16 + dm), in1=xT[:, dm, c0:c0 + T],
                                op0=ALU.mult, op1=ALU.add), r=[bpt, b_mod, b_xT[dm * NG + Q]], w=[b_xT[dm * NG + Q]])

                    norm(Q, lambda k: gs[:, 1, k:k + 1], lambda k: MB(24 + k),
                         lambda k: hT[:, k, :], b_h)

                    for jj in range(11):
                        slot, bsl = w_get()
                        for f2 in range(2):
                            jf = jj * 2 + f2
                            pgp, bpg = ps_next()
                            for k in range(8):
                                pe.op(lambda e, pgp=pgp, k=k, f2=f2, slot=slot: e.matmul(
                                    pgp, lhsT=slot[:, k * 512 + f2 * 128:k * 512 + (f2 + 1) * 128], rhs=hT[:, k, :],
                                    start=(k == 0), stop=(k == 7)), r=[bsl, b_h[k]], w=[bpg], sig=(k == 7))
                            pup, bpu = ps_next()
                            for k in range(8):
                                pe.op(lambda e, pup=pup, k=k, f2=f2, slot=slot: e.matmul(
                                    pup, lhsT=slot[:, k * 512 + 256 + f2 * 128:k * 512 + 256 + (f2 + 1) * 128],
                                    rhs=hT[:, k, :], start=(k == 0), stop=(k == 7)),
                                    r=[bsl, b_h[k]], w=[bpu], sig=(k == 7))
                            j = jf % 2
                            act.op(lambda e, pgp=pgp, j=j: e.activation(out=silb[:, j * T:(j + 1) * T], in_=pgp,
                                                                        func=AF.Silu), r=[bpg], w=[b_sil[j]])
                            dve.op(lambda e, pup=pup, j=j, jf=jf: e.tensor_tensor(
                                out=ffT[:, jf * T:(jf + 1) * T], in0=pup, in1=silb[:, j * T:(j + 1) * T],
                                op=ALU.mult), r=[bpu, b_sil[j]], w=[b_ff[jf]])
                    for dm in range(8):
                        slot, bsl = w_get()
                        pt, bpt = ps_next()
                        for jf in range(22):
                            pe.op(lambda e, pt=pt, jf=jf, slot=slot: e.matmul(
                                pt, lhsT=slot[:, jf * 128:(jf + 1) * 128], rhs=ffT[:, jf * T:(jf + 1) * T],
                                start=(jf == 0), stop=(jf == 21)), r=[bsl, b_ff[jf]], w=[bpt], sig=(jf == 21))
                        dve.op(lambda e, pt=pt, dm=dm: e.scalar_tensor_tensor(
                            out=xT[:, dm, c0:c0 + T], in0=pt, scalar=MB(40 + dm), in1=xT[:, dm, c0:c0 + T],
                            op0=ALU.mult, op1=ALU.add), r=[bpt, b_mod, b_xT[dm * NG + Q]], w=[b_xT[dm * NG + Q]])

            for Q in range(NG):
                if do_final:
                    norm(Q, lambda k: prm[:, P_FG + k:P_FG + k + 1], None,
                         lambda k: yT[:, k * T:(k + 1) * T], b_yT)
                    src = lambda k, tb: yT[:, k * T + tb * 128:k * T + (tb + 1) * 128]
                    srcb = lambda k: b_yT[k]
                else:
                    src = lambda k, tb, Q=Q: xT[:, k, Q * T + tb * 128:Q * T + (tb + 1) * 128]
                    srcb = lambda k, Q=Q: b_xT[k * NG + Q]
                for tb in range(4):
                    tg = Q * 4 + tb
                    j = tg % 2
                    for kh in range(2):
                        pt, bpt = ps_next()
                        for kk in range(4):
                            k = kh * 4 + kk
                            pe.op(lambda e, pt=pt, kk=kk, k=k, tb=tb, src=src: e.transpose(
                                pt[:, kk * 128:(kk + 1) * 128], src(k, tb), identf[:]),
                                r=[srcb(k), b_const], w=[bpt], sig=(kk == 3))
                        if kh == 0:
                            dve.op(lambda e, pt=pt, j=j: e.tensor_copy(out=xin[:, j * 1024:j * 1024 + 512], in_=pt),
                                   r=[bpt], w=[b_xin[j]])
                        else:
                            act.op(lambda e, pt=pt, j=j: e.activation(out=xin[:, j * 1024 + 512:(j + 1) * 1024],
                                                                      in_=pt, func=AF.Copy),
                                   r=[bpt], w=[b_xin[j]])
                    sp.dma(lambda e, tg=tg, j=j: e.dma_start(out=y_d[b, tg * 128:(tg + 1) * 128, :],
                                                             in_=xin[:, j * 1024:(j + 1) * 1024]),
                           ds_out[j], r=[b_xin[j]])
        sp.wait_for(b_xin)

        with nc.Block() as block:
            @block.tensor
            def _(e):
                pe.replay(e)

            @block.scalar
            def _(e):
                act.replay(e)

            @block.vector
            def _(e):
                dve.replay(e)

            @block.gpsimd
            def _(e):
                pool.replay(e)

            @block.sync
            def _(e):
                sp.replay(e)
    return nc


def _kp(w):
    K, C = w.shape
    return w.reshape(K // 128, 128, C).transpose(1, 0, 2)


def host_layout(inputs, layers=range(DEPTH)):
    w_in = inputs["w_in"]
    w_branch = inputs["w_branch"]
    w_out = inputs["w_out"]
    w_ffn_in = inputs["w_ffn_in"]
    w_ffn_out = inputs["w_ffn_out"]
    w_ada = inputs["w_ada"]
    wst = np.zeros((DEPTH, 128, WEL), np.float32)
    wada = np.zeros((DEPTH, 128, 8 * NMOD * D), np.float32)
    for l in layers:
        parts = []
        for s in range(6):
            parts.append(_kp(w_in[l][:, s * 512:(s + 1) * 512]).reshape(128, -1))
        for dm in range(8):
            blk = np.stack([_kp(w_in[l][:, 3072 + n * 1024 + dm * 128:3072 + n * 1024 + (dm + 1) * 128])
                            for n in range(3)], axis=2)
            parts.append(blk.reshape(128, -1))
            blk = np.stack([_kp(w_branch[l, n][:, dm * 128:(dm + 1) * 128]) for n in range(3)], axis=2)
            parts.append(blk.reshape(128, -1))
        for s in range(2):
            parts.append(_kp(w_out[l][:, s * 512:(s + 1) * 512]).reshape(128, -1))
        for jj in range(11):
            blk = np.stack([_kp(w_ffn_in[l][:, jj * 256:(jj + 1) * 256]),
                            _kp(w_ffn_in[l][:, DFF + jj * 256:DFF + (jj + 1) * 256])], axis=2)
            parts.append(blk.reshape(128, -1))
        for dm in range(8):
            parts.append(_kp(w_ffn_out[l][:, dm * 128:(dm + 1) * 128]).reshape(128, -1))
        wst[l] = np.concatenate(parts, axis=1)
        wa = _kp(w_ada[l])
        wada[l] = wa.reshape(128, 8, 12, 512).transpose(0, 2, 1, 3).reshape(128, -1)
    idx = np.arange(128)
    cm = np.zeros((128, 5, 128), np.float32)
    cm[:, 0] = np.eye(128, dtype=np.float32)
    cm[:, 1] = 1.0
    cm[:, 2] = np.where(idx[:, None] >= idx[None, :], -1.0, 0.0)
    cm[:, 3] = np.where(idx[:, None] < idx[None, :], 0.0, NEG)
    cm[:, 4] = np.where(idx[:, None] <= idx[None, :], 1.0, 0.0)
    cm = cm.reshape(128, 5 * 128)
    wsT = np.ascontiguousarray(inputs["gm_w_spatial"].transpose(3, 0, 1, 2)).reshape(128, -1)
    plw = np.ascontiguousarray(inputs["pool_w"].transpose(2, 0, 1, 3)).reshape(128, -1)
    bsr = np.ascontiguousarray(inputs["gm_b_spatial"]).reshape(1, -1)
    lnb = np.stack([np.broadcast_to(inputs["gm_ln_g"][:, None, :], (DEPTH, 128, 512)),
                    np.broadcast_to(inputs["gm_ln_b"][:, None, :], (DEPTH, 128, 512))], axis=2)
    lnb = np.ascontiguousarray(lnb).reshape(DEPTH, 128, 1024)
    prm = np.zeros((128, NPRM), np.float32)
    prm[:, P_G1:P_G1 + 32] = inputs["rms_g1"].reshape(DEPTH, 8, 128).transpose(2, 0, 1).reshape(128, 32)
    prm[:, P_G2:P_G2 + 32] = inputs["rms_g2"].reshape(DEPTH, 8, 128).transpose(2, 0, 1).reshape(128, 32)
    prm[:, P_FG:P_FG + 8] = inputs["final_g"].reshape(8, 128).T
    prm[:, P_BADA:P_BADA + 192] = inputs["b_ada"].reshape(DEPTH, 48, 128).transpose(2, 0, 1).reshape(128, 192)
    prm[:, P_PSC:P_PSC + 16] = inputs["pool_scale"].reshape(DEPTH, 4, 128).transpose(2, 0, 1).reshape(128, 16)
    pos = np.arange(16)
    invc = np.stack([1.0 / np.minimum(pos + 1, w) for w in (2, 4, 8, 16)]).astype(np.float32)
    prm[:, P_INVC:P_INVC + 64] = invc.reshape(1, 64)
    common = dict(wst=wst, wada=wada, cm=cm, wsT=wsT, plw=plw, bsr=bsr, lnb=lnb)
    return common, prm


def core_inputs(common, prm, x, c, b0, nseq):
    p = prm.copy()
    ct = np.zeros((128, 8, 4), np.float32)
    ct[:, :, :nseq] = c[b0:b0 + nseq].reshape(nseq, 8, 128).transpose(2, 1, 0)
    p[:, P_CT:P_CT + 32] = ct.reshape(128, 32)
    m = dict(common)
    m["prm"] = p
    m["x"] = np.ascontiguousarray(x[b0:b0 + nseq])
    return m


def kernel(**inputs):
    inputs = {k: np.asarray(v) for k, v in inputs.items()}
    x = inputs["x"]
    c = inputs["c"]
    Bt = x.shape[0]
    nseq = Bt // NCORES
    common, prm = host_layout(inputs)
    nc = build_program(nseq, list(range(DEPTH)), True)
    in_maps = [core_inputs(common, prm, x, c, i * nseq, nseq) for i in range(NCORES)]
    res = run_bass_kernel_spmd(nc, in_maps, core_ids=list(range(NCORES)))
    out = np.concatenate([r["y"] for r in res.results], axis=0)
    return out.astype(np.float32)
```
